# Optimizing a Trainium2 kernel written in Bass

```python
import jax, jax.numpy as jnp
from jax import lax
import numpy as np

D_MODEL = 1024
BATCH = 4
SEQ = 8192
DEPTH = 4
DEC_BATCH = 8
DEC_SEQ = 2048
PAST_LEN = 128

N_MIXERS = 2
N_GLA_LAYERS = (DEPTH + 1) // 2
N_SWA_LAYERS = DEPTH // 2
N_NORMS = 6
NORM_EPS = 1e-6

D_FF = 2816
FFN_RES = 0.5

GLA_HEADS = 4
GLA_DK = D_MODEL // 2 // GLA_HEADS
GLA_DV = D_MODEL // GLA_HEADS
GLA_QK = GLA_HEADS * GLA_DK
GLA_V = GLA_HEADS * GLA_DV
GLA_GATE_RANK = 16
GLA_TAU = 16.0
GLA_CHUNK = 64
GLA_IN = 2 * GLA_QK + 2 * GLA_V + 2 * GLA_GATE_RANK

SWA_Q_HEADS = 16
SWA_KV_HEADS = 4
SWA_GROUP = SWA_Q_HEADS // SWA_KV_HEADS
SWA_HD = 64
SWA_WINDOW = 128
SWA_BLOCK = 128
SWA_IN = (SWA_Q_HEADS + 2 * SWA_KV_HEADS) * SWA_HD
ROPE_THETA = 500000.0
ROPE_DIM = SWA_HD // 4
NEG_BIG = -1e30

kernel_name = "hybrid_gla_swa_macaron_encoder"


def rmsnorm(x, g):
    xf = x.astype(jnp.float32)
    xf = xf * lax.rsqrt(jnp.mean(xf * xf, axis=-1, keepdims=True) + NORM_EPS)
    return (xf * g.astype(jnp.float32)).astype(x.dtype)


def swiglu(x, w1, w2):
    gate, up = jnp.split(x @ w1, 2, axis=-1)
    return (jax.nn.silu(gate) * up) @ w2


def gla_chunked_causal(q, k, v, g):
    Bn, L, H, DK = q.shape
    DV = v.shape[-1]
    C = GLA_CHUNK
    N = L // C
    q = q.reshape(Bn, N, C, H, DK)
    k = k.reshape(Bn, N, C, H, DK)
    v = v.reshape(Bn, N, C, H, DV)
    b = jnp.cumsum(g.reshape(Bn, N, C, H, DK), axis=2)
    b_last = b[:, :, -1]
    q_dec = q * jnp.exp(b)
    k_inv = k * jnp.exp(-b)
    k_end = k * jnp.exp(b_last[:, :, None] - b)
    mask = jnp.tril(jnp.ones((C, C), dtype=bool))
    a = jnp.einsum('bnchk,bnshk->bnhcs', q_dec, k_inv)
    a = jnp.where(mask, a, 0.0)
    o_intra = jnp.einsum('bnhcs,bnshv->bnchv', a, v)

    def step(S, inp):
        q_c, k_c, v_c, dec = inp
        o = jnp.einsum('bchk,bhkv->bchv', q_c, S)
        S = dec[..., None] * S + jnp.einsum('bchk,bchv->bhkv', k_c, v_c)
        return S, o

    S0 = jnp.zeros((Bn, H, DK, DV), jnp.float32)
    xs = (jnp.moveaxis(q_dec, 1, 0), jnp.moveaxis(k_end, 1, 0),
          jnp.moveaxis(v, 1, 0), jnp.moveaxis(jnp.exp(b_last), 1, 0))
    _, o_inter = lax.scan(step, S0, xs)
    o = o_intra + jnp.moveaxis(o_inter, 0, 1)
    return o.reshape(Bn, L, H, DV)


def gla_mixer(x, w_in, w_gate_f, b_gate_f, w_gate_b, b_gate_b, g_onorm, w_out):
    Bn, L, _ = x.shape
    h = x @ w_in
    splits = [GLA_QK, 2 * GLA_QK, 2 * GLA_QK + GLA_V, 2 * GLA_QK + 2 * GLA_V,
              2 * GLA_QK + 2 * GLA_V + GLA_GATE_RANK]
    q, k, v, r, gd_f, gd_b = jnp.split(h, splits, axis=-1)
    q = q.reshape(Bn, L, GLA_HEADS, GLA_DK).astype(jnp.float32) * (GLA_DK ** -0.5)
    k = k.reshape(Bn, L, GLA_HEADS, GLA_DK).astype(jnp.float32)
    v = v.reshape(Bn, L, GLA_HEADS, GLA_DV).astype(jnp.float32)
    g_f = jax.nn.log_sigmoid((gd_f @ w_gate_f + b_gate_f).astype(jnp.float32)) / GLA_TAU
    g_b = jax.nn.log_sigmoid((gd_b @ w_gate_b + b_gate_b).astype(jnp.float32)) / GLA_TAU
    g_f = g_f.reshape(Bn, L, GLA_HEADS, GLA_DK)
    g_b = g_b.reshape(Bn, L, GLA_HEADS, GLA_DK)
    o_f = gla_chunked_causal(q, k, v, g_f)
    o_b = jnp.flip(gla_chunked_causal(jnp.flip(q, 1), jnp.flip(k, 1), jnp.flip(v, 1),
                                      jnp.flip(g_b, 1)), axis=1)
    o = o_f + o_b
    o = o * lax.rsqrt(jnp.mean(o * o, axis=-1, keepdims=True) + NORM_EPS)
    o = o.reshape(Bn, L, GLA_V) * g_onorm.astype(jnp.float32)
    o = o.astype(x.dtype) * jax.nn.silu(r)
    return o @ w_out


def rope_partial(x, pos):
    half = ROPE_DIM // 2
    inv_freq = ROPE_THETA ** (-(jnp.arange(half, dtype=jnp.float32) * 2.0 / ROPE_DIM))
    ang = pos.astype(jnp.float32)[:, None] * inv_freq[None, :]
    cos = jnp.cos(ang)[None, :, None, :]
    sin = jnp.sin(ang)[None, :, None, :]
    xr = x[..., :ROPE_DIM].astype(jnp.float32)
    x1, x2 = xr[..., :half], xr[..., half:]
    rot = jnp.concatenate([x1 * cos - x2 * sin, x2 * cos + x1 * sin], axis=-1)
    return jnp.concatenate([rot.astype(x.dtype), x[..., ROPE_DIM:]], axis=-1)


def swa_mixer(x, w_in, sinks, w_out):
    Bn, L, _ = x.shape
    h = x @ w_in
    q, k, v = jnp.split(h, [SWA_Q_HEADS * SWA_HD, (SWA_Q_HEADS + SWA_KV_HEADS) * SWA_HD], axis=-1)
    q = q.reshape(Bn, L, SWA_Q_HEADS, SWA_HD)
    k = k.reshape(Bn, L, SWA_KV_HEADS, SWA_HD)
    v = v.reshape(Bn, L, SWA_KV_HEADS, SWA_HD)
    pos = jnp.arange(L)
    q = rope_partial(q, pos) * (SWA_HD ** -0.5)
    k = rope_partial(k, pos)
    N = L // SWA_BLOCK
    q_blk = q.reshape(Bn, N, SWA_BLOCK, SWA_KV_HEADS, SWA_GROUP, SWA_HD)
    pad = ((0, 0), (SWA_BLOCK, SWA_BLOCK), (0, 0), (0, 0))
    k_pad = jnp.pad(k, pad).reshape(Bn, N + 2, SWA_BLOCK, SWA_KV_HEADS, SWA_HD)
    v_pad = jnp.pad(v, pad).reshape(Bn, N + 2, SWA_BLOCK, SWA_KV_HEADS, SWA_HD)
    k_band = jnp.concatenate([k_pad[:, 0:N], k_pad[:, 1:N + 1], k_pad[:, 2:N + 2]], axis=2)
    v_band = jnp.concatenate([v_pad[:, 0:N], v_pad[:, 1:N + 1], v_pad[:, 2:N + 2]], axis=2)
    s = jnp.einsum('bnqhgd,bnkhd->bnhgqk', q_blk, k_band).astype(jnp.float32)
    blk = jnp.arange(N)[:, None] * SWA_BLOCK
    qpos = blk + jnp.arange(SWA_BLOCK)[None, :]
    kpos = blk - SWA_BLOCK + jnp.arange(3 * SWA_BLOCK)[None, :]
    valid = (jnp.abs(qpos[:, :, None] - kpos[:, None, :]) <= SWA_WINDOW) \
        & ((kpos >= 0) & (kpos < L))[:, None, :]
    s = jnp.where(valid[None, :, None, None, :, :], s, NEG_BIG)
    sink = jnp.broadcast_to(
        sinks.astype(jnp.float32).reshape(1, 1, SWA_KV_HEADS, SWA_GROUP, 1, 1), s.shape[:-1] + (1,))
    p = jax.nn.softmax(jnp.concatenate([s, sink], axis=-1), axis=-1)[..., :-1]
    o = jnp.einsum('bnhgqk,bnkhd->bnqhgd', p.astype(v.dtype), v_band)
    o = o.reshape(Bn, L, SWA_Q_HEADS * SWA_HD)
    return o @ w_out


def trunk(x, norm_g, ffn_w1, ffn_w2, gla_w_in, gla_w_gate_f, gla_b_gate_f, gla_w_gate_b,
          gla_b_gate_b, gla_onorm, gla_w_out, swa_w_in, swa_sinks, swa_w_out):
    for i in range(DEPTH):
        g = norm_g[i]
        h = swiglu(rmsnorm(x, g[0]), ffn_w1[i, 0], ffn_w2[i, 0])
        x = x + FFN_RES * rmsnorm(h, g[1])
        h = rmsnorm(x, g[2])
        j = i // N_MIXERS
        if i % N_MIXERS == 0:
            h = gla_mixer(h, gla_w_in[j], gla_w_gate_f[j], gla_b_gate_f[j], gla_w_gate_b[j],
                          gla_b_gate_b[j], gla_onorm[j], gla_w_out[j])
        else:
            h = swa_mixer(h, swa_w_in[j], swa_sinks[j], swa_w_out[j])
        x = x + rmsnorm(h, g[3])
        h = swiglu(rmsnorm(x, g[4]), ffn_w1[i, 1], ffn_w2[i, 1])
        x = x + FFN_RES * rmsnorm(h, g[5])
    return x


def setup_inputs(seed: int = 0) -> dict:
    key = jax.random.key(seed)
    ks = jax.random.split(key, 16)
    f32 = jnp.float32
    nrm = lambda k, shape, scale: jax.random.normal(k, shape, f32) * scale
    return {
        "x_prompt": nrm(ks[0], (BATCH, SEQ, D_MODEL), 1.0),
        "x_sample": nrm(ks[1], (DEC_BATCH, DEC_SEQ, D_MODEL), 1.0),
        "norm_g": 1.0 + nrm(ks[2], (DEPTH, N_NORMS, D_MODEL), 0.05),
        "ffn_w1": nrm(ks[3], (DEPTH, 2, D_MODEL, 2 * D_FF), D_MODEL ** -0.5),
        "ffn_w2": nrm(ks[4], (DEPTH, 2, D_FF, D_MODEL), D_FF ** -0.5),
        "gla_w_in": nrm(ks[5], (N_GLA_LAYERS, D_MODEL, GLA_IN), D_MODEL ** -0.5),
        "gla_w_gate_f": nrm(ks[6], (N_GLA_LAYERS, GLA_GATE_RANK, GLA_QK), GLA_GATE_RANK ** -0.5),
        "gla_b_gate_f": nrm(ks[7], (N_GLA_LAYERS, GLA_QK), 0.1),
        "gla_w_gate_b": nrm(ks[8], (N_GLA_LAYERS, GLA_GATE_RANK, GLA_QK), GLA_GATE_RANK ** -0.5),
        "gla_b_gate_b": nrm(ks[9], (N_GLA_LAYERS, GLA_QK), 0.1),
        "gla_onorm": 1.0 + nrm(ks[10], (N_GLA_LAYERS, GLA_V), 0.05),
        "gla_w_out": nrm(ks[11], (N_GLA_LAYERS, GLA_V, D_MODEL), GLA_V ** -0.5),
        "swa_w_in": nrm(ks[12], (N_SWA_LAYERS, D_MODEL, SWA_IN), D_MODEL ** -0.5),
        "swa_sinks": nrm(ks[13], (N_SWA_LAYERS, SWA_Q_HEADS), 1.0),
        "swa_w_out": nrm(ks[14], (N_SWA_LAYERS, SWA_Q_HEADS * SWA_HD, D_MODEL), (SWA_Q_HEADS * SWA_HD) ** -0.5),
    }


def reference(x_prompt, x_sample, norm_g, ffn_w1, ffn_w2, gla_w_in, gla_w_gate_f, gla_b_gate_f,
              gla_w_gate_b, gla_b_gate_b, gla_onorm, gla_w_out, swa_w_in, swa_sinks, swa_w_out):
    y_prompt = trunk(x_prompt, norm_g, ffn_w1, ffn_w2, gla_w_in, gla_w_gate_f, gla_b_gate_f,
                     gla_w_gate_b, gla_b_gate_b, gla_onorm, gla_w_out, swa_w_in, swa_sinks, swa_w_out)
    y_sample = trunk(x_sample, norm_g, ffn_w1, ffn_w2, gla_w_in, gla_w_gate_f, gla_b_gate_f,
                     gla_w_gate_b, gla_b_gate_b, gla_onorm, gla_w_out, swa_w_in, swa_sinks, swa_w_out)
    return (y_prompt, y_sample)
```

```python
import numpy as np
import ml_dtypes
from contextlib import ExitStack
import concourse.bass as bass
import concourse.mybir as mybir
from concourse.bass_utils import run_bass_kernel_spmd

F32 = mybir.dt.float32
BF16 = mybir.dt.bfloat16
ALU = mybir.AluOpType
AF = mybir.ActivationFunctionType
AX = mybir.AxisListType

D = 1024
DFF = 2816
NFC = DFF // 128
EPS = 1e-6
SEM_ROLL = 30000


class Dep:
    __slots__ = ("eng", "sem", "val")

    def __init__(self, eng):
        self.eng = eng
        self.sem = None
        self.val = 0


class Tok:
    __slots__ = ("w", "r", "sems")

    def __init__(self):
        self.w = None
        self.r = []
        self.sems = {}


class Eng:
    def __init__(self, prog, name, h, selfsafe=False):
        self.prog = prog
        self.name = name
        self.h = h
        self.selfsafe = selfsafe
        self.sem = None
        self.semv = 0
        self.waited = {}
        self.pending = []
        self.nsem = 0
        self.last = None

    def newsem(self):
        self.sem = self.prog.nc.alloc_semaphore(f"e_{self.name}_{self.nsem}")
        self.nsem += 1
        self.semv = 0


class Prog:
    def __init__(self, nc):
        self.nc = nc
        self.pe = Eng(self, "pe", nc.tensor, selfsafe=True)
        self.act = Eng(self, "act", nc.scalar)
        self.dve = Eng(self, "dve", nc.vector)
        self.pool = Eng(self, "pool", nc.gpsimd)
        self.sp = Eng(self, "sp", nc.sync)
        self.engs = [self.pe, self.act, self.dve, self.pool, self.sp]
        for e in self.engs:
            e.newsem()
        self.dma_toks = []
        self.free_sems = {"hw": [], "sw": []}
        self.nsem_alloc = 0
        self.nins = 0
        self.log = {e.name: [] for e in self.engs}

    def _wait_deps(self, eng, deps):
        need = {}
        for d in deps:
            if d is None:
                continue
            if d.eng is eng and eng.selfsafe:
                continue
            assert d.sem is not None, "dependency on an unsignalled instruction"
            key = d.sem
            if eng.waited.get(key.num, 0) >= d.val:
                continue
            if need.get(key.num, (None, 0))[1] < d.val:
                need[key.num] = (key, d.val)
        for num, (sem, val) in need.items():
            eng.h.wait_ge(sem, val)
            eng.waited[num] = val
            self.log[eng.name].append(("w", num, val))
            self.nins += 1

    def _collect(self, reads, writes):
        deps = []
        for t in reads:
            if t.w is not None:
                deps.append(t.w)
        for t in writes:
            if t.w is not None:
                deps.append(t.w)
            deps.extend(t.r)
        return deps

    def _record(self, dep, reads, writes):
        for t in reads:
            t.r.append(dep)
        for t in writes:
            t.w = dep
            t.r = []

    def op(self, eng, fn, reads=(), writes=(), sig=True):
        self._wait_deps(eng, self._collect(reads, writes))
        ins = fn()
        self.nins += 1
        dep = Dep(eng)
        if sig:
            if eng.semv >= SEM_ROLL:
                eng.newsem()
            eng.semv += 1
            ins.then_inc(eng.sem, 1)
            self.log[eng.name].append(("i", eng.sem.num, 1))
            dep.sem = eng.sem
            dep.val = eng.semv
            for p in eng.pending:
                p.sem = dep.sem
                p.val = dep.val
            eng.pending = []
            eng.last = dep
        else:
            eng.pending.append(dep)
        self._record(dep, reads, writes)
        return dep

    def dma(self, q, out, in_, reads, writes, stok):
        self._wait_deps(q, self._collect(reads, writes))
        sw = q is self.pool
        key = "sw" if sw else "hw"
        cur = stok.sems.get(key)
        if cur is None or cur[1] >= SEM_ROLL:
            fl = self.free_sems[key]
            while fl and fl[-1][1] >= SEM_ROLL:
                fl.pop()
            if fl:
                cur = list(fl.pop())
            else:
                self.nsem_alloc += 1
                cur = [self.nc.alloc_semaphore(f"d{key}_{self.nsem_alloc}"), 0]
            stok.sems[key] = cur
            if stok not in self.dma_toks:
                self.dma_toks.append(stok)
        cur[1] += 16
        q.h.dma_start(out=out, in_=in_).then_inc(cur[0], 16)
        self.log[q.name].append(("i", cur[0].num, 16))
        self.nins += 1
        dep = Dep(None)
        dep.sem = cur[0]
        dep.val = cur[1]
        self._record(dep, reads, writes)
        return dep

    def barrier(self, toks=()):
        deps = [e.last for e in self.engs if e.last is not None]
        for t in list(self.dma_toks) + list(toks):
            if t.w is not None:
                deps.append(t.w)
            deps.extend(t.r)
        for e in self.engs:
            assert not e.pending
            self._wait_deps(e, deps)
        for t in list(self.dma_toks) + list(toks):
            t.w = None
            t.r = []
        for t in self.dma_toks:
            for key, cur in t.sems.items():
                self.free_sems[key].append((cur[0], cur[1]))
            t.sems = {}
        self.dma_toks = []


def simulate(P):
    pos = {k: 0 for k in P.log}
    sems = {}
    prog = True
    while prog:
        prog = False
        for k, lst in P.log.items():
            while pos[k] < len(lst):
                kind, num, val = lst[pos[k]]
                if kind == "w":
                    if sems.get(num, 0) < val:
                        break
                else:
                    sems[num] = sems.get(num, 0) + val
                pos[k] += 1
                prog = True
    stuck = {k: (pos[k], len(l), l[pos[k]]) for k, l in P.log.items() if pos[k] < len(l)}
    return stuck, sems


def run_interleaved(gens):
    gens = [g for g in gens if g is not None]
    while gens:
        for g in list(gens):
            try:
                next(g)
            except StopIteration:
                gens.remove(g)


class RPool:
    def __init__(self, tiles):
        self.tiles = tiles
        self.toks = [Tok() for _ in tiles]
        self.i = 0

    def get(self):
        i = self.i
        self.i = (i + 1) % len(self.tiles)
        return self.tiles[i], self.toks[i]


class Ctx:
    pass


_uid = [0]


def sb(nc, es, name, shape, dtype):
    _uid[0] += 1
    return es.enter_context(nc.sbuf_tensor(f"{name}_{_uid[0]}", shape, dtype))


def tiles(nc, es, name, shape, dtype, n):
    return RPool([sb(nc, es, f"{name}{i}", shape, dtype) for i in range(n)])


def setup_globals(nc, P, C):
    C.psT = RPool([nc.alloc_psum_tensor(f"psT{i}", [128, 1024], BF16) for i in range(2)])
    C.psS = nc.alloc_psum_tensor("psS", [128, 4, 512], F32)
    C.psA = RPool([C.psS[:, i, :] for i in range(4)])
    C.psO = RPool([nc.alloc_psum_tensor(f"psO{i}", [128, 512], F32) for i in range(2)])
    C.idt = nc.alloc_sbuf_tensor("idt", [128, 128], BF16)
    C.t_idt = Tok()
    P.dma(P.sp, C.idt[:], C.ident, [], [C.t_idt], C.t_idt)


def conv_stream(nc, P, C, jobs, st, cb, cast_engs):
    k = 0
    for srcs, dst, tok in jobs:
        stt, t_st = st.get()
        cbt, t_cb = cb.get()
        a, b = dst.shape[1], dst.shape[2]
        assert a * b <= 2048
        sv = stt[:, 0:a * b].rearrange("p (a b) -> p a b", a=a)
        cv = cbt[:, 0:a * b].rearrange("p (a b) -> p a b", a=a)
        for (lo, hi), src in srcs:
            P.dma(P.sp, sv[:, :, lo:hi], src, [], [t_st], t_st)
        yield
        e = cast_engs[k % len(cast_engs)]
        k += 1
        if e is P.act:
            P.op(P.act, lambda: nc.scalar.copy(out=cv, in_=sv), [t_st], [t_cb])
        elif e is P.pool:
            P.op(P.pool, lambda: nc.gpsimd.tensor_copy(out=cv, in_=sv), [t_st], [t_cb])
        else:
            P.op(P.dve, lambda: nc.vector.tensor_copy(out=cv, in_=sv), [t_st], [t_cb])
        yield
        P.dma(P.pool, dst, cv, [t_cb], [tok], t_cb)
        yield


def convert_weights(nc, P, C, jobs):
    with ExitStack() as es:
        st = tiles(nc, es, "cv_st", [128, 2048], F32, 4)
        cb = tiles(nc, es, "cv_cb", [128, 2048], BF16, 4)
        run_interleaved([conv_stream(nc, P, C, jobs, st, cb, [P.dve, P.act])])
        P.barrier([j[2] for j in jobs[:1]])


def rms_stats(nc, P, C, src_ap, t_src, coef, small):
    junk, t_junk = C.junk.get()
    st, t_st = small.get()
    P.op(P.act, lambda: nc.scalar.activation(out=junk[:], in_=src_ap, func=AF.Square, scale=1.0 / 32, accum_out=st[:, 0:1]), [t_src], [t_junk, t_st])
    P.op(P.act, lambda: nc.scalar.activation(out=st[:, 1:2], in_=st[:, 0:1], func=AF.Ln, scale=1.0 / coef ** 2, bias=EPS / coef ** 2), [t_st], [t_st])
    P.op(P.act, lambda: nc.scalar.activation(out=st[:, 2:3], in_=st[:, 1:2], func=AF.Exp, scale=-0.5), [t_st], [t_st])
    return st[:, 2:3], t_st


def norm_transpose(nc, P, C, xt, t_xt, gbc, t_g, dstT, t_dst, small, xnp):
    rstd, t_r = rms_stats(nc, P, C, xt[:], t_xt, 1.0, small)
    xn, t_xn = xnp.get()
    P.op(P.dve, lambda: nc.vector.scalar_tensor_tensor(out=xn[:], in0=xt[:], scalar=rstd, in1=gbc[:], op0=ALU.mult, op1=ALU.mult), [t_xt, t_r, t_g], [t_xn])
    transpose8(nc, P, C, xn, t_xn, dstT, t_dst)


def transpose8(nc, P, C, xn, t_xn, dstT, t_dst, eng=None):
    ps, t_ps = C.psT.get()
    for kc in range(8):
        P.op(P.pe, lambda kc=kc: nc.tensor.transpose(out=ps[:, kc * 128:(kc + 1) * 128], in_=xn[:, kc * 128:(kc + 1) * 128], identity=C.idt[:]),
             [t_xn, C.t_idt], [t_ps], sig=(kc == 7))
    P.op(P.act, lambda: nc.scalar.copy(out=dstT, in_=ps[:].rearrange("p (k t) -> p k t", k=8)), [t_ps], [t_dst])


def post_residual(nc, P, C, y0, t_y0, y1, t_y1, gbc, t_g, xt, t_xt, coef, small, xop, dst_rows):
    ys, t_ys = C.ysp.get()
    P.op(P.act, lambda: nc.scalar.copy(out=ys[:, 0:512], in_=y0[:]), [t_y0], [t_ys])
    P.op(P.act, lambda: nc.scalar.copy(out=ys[:, 512:1024], in_=y1[:]), [t_y1], [t_ys])
    rstd, t_r = rms_stats(nc, P, C, ys[:], t_ys, coef, small)
    xo, t_xo = xop.get()
    P.op(P.dve, lambda: nc.vector.tensor_tensor(out=xo[:], in0=ys[:], in1=gbc[:], op=ALU.mult), [t_ys, t_g], [t_xo])
    P.op(P.dve, lambda: nc.vector.scalar_tensor_tensor(out=xo[:], in0=xo[:], scalar=rstd, in1=xt[:], op0=ALU.mult, op1=ALU.add), [t_xo, t_r, t_xt], [t_xo])
    P.dma(P.pool, dst_rows, xo[:], [t_xo], [C.t_xdst], t_xo)


def load_gain(nc, P, C, es, name, gvec):
    g = sb(nc, es, name, [128, 1024], F32)
    t = Tok()
    P.dma(P.sp, g[:], gvec.partition_broadcast(128), [], [t], t)
    return g, t


def ffn_phase(nc, P, C, src, dst, w1s, w2s, g_pre, g_post, TB=1024, bg_jobs=()):
    NT = TB // 128
    NH = TB // 512
    with ExitStack() as es:
        w2b = sb(nc, es, "w2b", [128, NFC, 1024], BF16)
        t_w2 = [Tok() for _ in range(NFC // 2)]
        xnTs = [sb(nc, es, f"xnT{i}", [128, 8, TB], BF16) for i in range(2)]
        t_xnTs = [[Tok() for _ in range(NT)] for _ in range(2)]
        aT = sb(nc, es, "aT", [128, NFC, TB], BF16)
        t_aT = [[Tok() for _ in range(NH)] for _ in range(NFC)]
        slabs = tiles(nc, es, "slab", [128, 8, 256], BF16, 3)
        xtp = tiles(nc, es, "xt", [128, 1024], F32, 3)
        xnp = tiles(nc, es, "xn", [128, 1024], BF16, 2)
        C.junk = tiles(nc, es, "junk", [128, 1024], BF16, 1)
        sgp = tiles(nc, es, "sg", [128, 512], F32, 3)
        xop = tiles(nc, es, "xo", [128, 1024], F32, 2)
        C.ysp = tiles(nc, es, "ys", [128, 1024], F32, 2)
        small = tiles(nc, es, "small", [128, 16], F32, 6)
        gpre, t_gpre = load_gain(nc, P, C, es, "gpre", g_pre)
        gpost, t_gpost = load_gain(nc, P, C, es, "gpost", g_post)
        for i in range(NFC // 2):
            P.dma(P.sp, w2b[:, 2 * i:2 * i + 2, :], w2s[:, 2 * i:2 * i + 2, :], [C.t_wscr], [t_w2[i]], t_w2[i])
        nblk = C.NTOK // TB
        bg = None
        if bg_jobs:
            cst = tiles(nc, es, "bg_st", [128, 2048], F32, 2)
            ccb = tiles(nc, es, "bg_cb", [128, 2048], BF16, 2)
            bg = conv_stream(nc, P, C, list(bg_jobs), cst, ccb, [P.pool])
        bg_steps = (3 * len(bg_jobs) + nblk - 1) // nblk if bg_jobs else 0

        def bg_chunk():
            for _ in range(bg_steps):
                try:
                    next(bg)
                except StopIteration:
                    return
                yield

        def stageA(blk):
            r0 = blk * TB
            xnT = xnTs[blk % 2]
            for tt in range(NT):
                xt, t_xt = xtp.get()
                P.dma(P.sp, xt[:], src[r0 + tt * 128:r0 + (tt + 1) * 128, :], [C.t_xsrc], [t_xt], t_xt)
                rstd, t_r = rms_stats(nc, P, C, xt[:], t_xt, 1.0, small)
                xn, t_xn = xnp.get()
                P.op(P.dve, lambda: nc.vector.scalar_tensor_tensor(out=xn[:], in0=xt[:], scalar=rstd, in1=gpre[:], op0=ALU.mult, op1=ALU.mult), [t_xt, t_r, t_gpre], [t_xn])
                yield
                transpose8(nc, P, C, xn, t_xn, xnT[:, :, tt * 128:(tt + 1) * 128], t_xnTs[blk % 2][tt])
                yield

        def stageB(blk):
            xnT = xnTs[blk % 2]
            t_xnT = t_xnTs[blk % 2]
            pend = []

            def load_slab(fp):
                sl, t_sl = slabs.get()
                P.dma(P.sp, sl[:], w1s[fp], [C.t_wscr], [t_sl], t_sl)
                return sl, t_sl
            pend.append(load_slab(0))
            pend.append(load_slab(1))
            for fp in range(NFC):
                sl, t_sl = pend.pop(0)
                if fp + 2 < NFC:
                    pend.append(load_slab(fp + 2))
                for h in range(NH):
                    rd = [t_sl] + t_xnT[h * 4:(h + 1) * 4]
                    pg, t_pg = C.psA.get()
                    for kc in range(8):
                        P.op(P.pe, lambda kc=kc: nc.tensor.matmul(pg[:], lhsT=sl[:, kc, 0:128], rhs=xnT[:, kc, h * 512:(h + 1) * 512], start=(kc == 0), stop=(kc == 7)),
                             rd, [t_pg], sig=(kc == 7))
                    pu, t_pu = C.psA.get()
                    for kc in range(8):
                        P.op(P.pe, lambda kc=kc: nc.tensor.matmul(pu[:], lhsT=sl[:, kc, 128:256], rhs=xnT[:, kc, h * 512:(h + 1) * 512], start=(kc == 0), stop=(kc == 7)),
                             rd, [t_pu], sig=(kc == 7))
                    sg, t_sg = sgp.get()
                    P.op(P.act, lambda: nc.scalar.activation(out=sg[:], in_=pg[:], func=AF.Silu), [t_pg], [t_sg])
                    P.op(P.dve, lambda: nc.vector.tensor_tensor(out=aT[:, fp, h * 512:(h + 1) * 512], in0=pu[:], in1=sg[:], op=ALU.mult), [t_pu, t_sg], [t_aT[fp][h]])
                    yield

        def stageC(blk):
            r0 = blk * TB
            for tt in range(NT):
                xt, t_xt = xtp.get()
                rows = slice(r0 + tt * 128, r0 + (tt + 1) * 128)
                P.dma(P.sp, xt[:], src[rows, :], [C.t_xsrc], [t_xt], t_xt)
                ys = []
                for dh in range(2):
                    y, t_y = C.psA.get()
                    for fc in range(NFC):
                        P.op(P.pe, lambda fc=fc: nc.tensor.matmul(y[:], lhsT=aT[:, fc, tt * 128:(tt + 1) * 128], rhs=w2b[:, fc, dh * 512:(dh + 1) * 512], start=(fc == 0), stop=(fc == NFC - 1)),
                             [t_aT[fc][tt // 4], t_w2[fc // 2]], [t_y], sig=(fc == NFC - 1))
                    ys.append((y, t_y))
                post_residual(nc, P, C, ys[0][0], ys[0][1], ys[1][0], ys[1][1], gpost, t_gpost, xt, t_xt, 0.5, small, xop, dst[rows, :])

        run_interleaved([stageA(0)])
        for blk in range(nblk):
            run_interleaved([stageB(blk), stageA(blk + 1) if blk + 1 < nblk else None, bg_chunk() if bg is not None else None])
            stageC(blk)
        if bg is not None:
            for _ in bg:
                pass
        P.barrier([C.t_xdst, C.t_xsrc, C.t_wscr])


GLA_IN = 3104
DK = 128
DV = 256
NH = 4


def mat_jobs(w, scr, ncols):
    v = w.rearrange("(kc p) f -> p kc f", p=128)
    jobs = []
    for c0 in range(0, ncols, 256):
        c1 = min(ncols, c0 + 256)
        jobs.append(([((0, c1 - c0), v[:, :, c0:c1])], scr[:, :, c0:c1], "gen"))
    return jobs


def gla_sweep(nc, P, C, es, W, direction, src, dst, of_dram):
    fwd = direction == 0
    NTILE = C.NTOK // 128
    order = list(range(NTILE)) if fwd else list(range(NTILE - 1, -1, -1))
    chunks = [0, 1] if fwd else [1, 0]
    gdc = 3072 + 16 * direction
    for h in range(NH):
        P.op(P.dve, lambda h=h: nc.vector.memset(W.T[h][:], 0.0), [], [W.t_T[h]])
    d0, t_d0 = W.decp.get()
    P.op(P.dve, lambda: nc.vector.memset(d0[:], 1.0), [], [t_d0])
    state = dict(dprev=(d0, t_d0))

    def stage1a(tl, S):
        rows = slice(tl * 128, (tl + 1) * 128)
        xt, t_xt = W.xtp.get()
        P.dma(P.sp, xt[:], src[rows, :], [C.t_xsrc], [t_xt], t_xt)
        xnT, t_xnT = W.xnTp.get()
        norm_transpose(nc, P, C, xt, t_xt, W.g2, W.t_g2, xnT[:], t_xnT, W.small, W.xnp)
        yield
        rdx = [t_xnT, W.t_win]
        gp, t_gp = C.psA.get()
        for kc in range(8):
            P.op(P.pe, lambda kc=kc: nc.tensor.matmul(gp[0:16, 0:128], lhsT=W.win[:, kc, gdc:gdc + 16], rhs=xnT[:, kc, :], start=(kc == 0), stop=(kc == 7)), rdx, [t_gp], sig=(kc == 7))
        ga, t_ga = W.gap.get()
        P.op(P.act, lambda: nc.scalar.copy(out=ga[0:16, :], in_=gp[0:16, 0:128]), [t_gp], [t_ga])
        yield
        zp, t_zp = C.psA.get()
        P.op(P.pe, lambda: nc.tensor.matmul(zp[:], lhsT=ga[0:17, :], rhs=W.wg[direction][0:17, :], start=True, stop=True), [t_ga, W.t_wg], [t_zp])
        ee, t_ee = W.ep.get()
        P.op(P.act, lambda: nc.scalar.activation(out=ee[:], in_=zp[:], func=AF.Exp, scale=-1.0), [t_zp], [t_ee])
        ll, t_ll = W.lp.get()
        P.op(P.act, lambda: nc.scalar.activation(out=ll[:], in_=ee[:], func=AF.Ln, bias=1.0), [t_ee], [t_ll])
        yield
        bt, t_bt = C.psA.get()
        P.op(P.pe, lambda: nc.tensor.matmul(bt[:], lhsT=W.tri[direction][:], rhs=ll[:], start=True, stop=True), [W.t_tri, t_ll], [t_bt])
        bd, t_bd = C.psA.get()
        for h in range(NH):
            P.op(P.pe, lambda h=h: nc.tensor.matmul(bd[:, h * 16:(h + 1) * 16], lhsT=ll[:, h * 128:(h + 1) * 128], rhs=W.tri2[:], start=True, stop=True), [W.t_tri, t_ll], [t_bd], sig=(h == NH - 1))
        dec, t_dec = W.decp.get()
        P.op(P.act, lambda: nc.scalar.activation(out=dec[:].rearrange("p (h c) -> p h c", c=2), in_=bd[:, 0:64].rearrange("p (h e) -> p h e", e=16)[:, :, 0:2], func=AF.Exp), [t_bd], [t_dec])
        yield
        if C.dbg == "h1":
            return None
        eit, t_eit = W.eitp.get()
        P.op(P.act, lambda: nc.scalar.activation(out=eit[:], in_=bt[:], func=AF.Exp, scale=-1.0), [t_bt], [t_eit])
        ebt, t_ebt = W.ebTp.get()
        P.op(P.act, lambda: nc.scalar.activation(out=ebt[:], in_=bt[:], func=AF.Exp), [t_bt], [t_ebt])
        yield
        S.update(tl=tl, rows=rows, xt=xt, t_xt=t_xt, xnT=xnT, t_xnT=t_xnT, rdx=rdx, eit=eit, t_eit=t_eit, ebt=ebt, t_ebt=t_ebt, dec=dec, t_dec=t_dec)
        yield

    def stage1b(S):
        xnT = S['xnT']; t_xnT = S['t_xnT']; rdx = S['rdx']; eit = S['eit']; t_eit = S['t_eit']; ebt = S['ebt']; t_ebt = S['t_ebt']
        kt, t_kt = C.psA.get()
        for kc in range(8):
            P.op(P.pe, lambda kc=kc: nc.tensor.matmul(kt[:], lhsT=xnT[:, kc, :], rhs=W.win[:, kc, 512:1024], start=(kc == 0), stop=(kc == 7)), rdx, [t_kt], sig=(kc == 7))
        kit, t_kit = W.kitp.get()
        P.op(P.dve, lambda: nc.vector.tensor_tensor(out=kit[:], in0=kt[:], in1=eit[:], op=ALU.mult), [t_kt, t_eit], [t_kit])
        yield
        qt, t_qt = C.psA.get()
        for kc in range(8):
            P.op(P.pe, lambda kc=kc: nc.tensor.matmul(qt[:], lhsT=xnT[:, kc, :], rhs=W.win[:, kc, 0:512], start=(kc == 0), stop=(kc == 7)), rdx, [t_qt], sig=(kc == 7))
        qdt, t_qdt = W.qdtp.get()
        P.op(P.dve, lambda: nc.vector.scalar_tensor_tensor(out=qdt[:], in0=qt[:], scalar=float(DK ** -0.5), in1=ebt[:], op0=ALU.mult, op1=ALU.mult), [t_qt, t_ebt], [t_qdt])
        yield
        vb, t_vb = W.vbp.get()
        for j in range(2):
            vp, t_vp = C.psA.get()
            for kc in range(8):
                P.op(P.pe, lambda kc=kc: nc.tensor.matmul(vp[:], lhsT=xnT[:, kc, :], rhs=W.win[:, kc, 1024 + j * 512:1536 + j * 512], start=(kc == 0), stop=(kc == 7)), rdx, [t_vp], sig=(kc == 7))
            P.op(P.act, lambda: nc.scalar.copy(out=vb[:, j * 512:(j + 1) * 512], in_=vp[:]), [t_vp], [t_vb])
            yield
        if C.dbg == "h2":
            return None
        pst, t_pst = C.psT.get()
        for h in range(NH):
            P.op(P.pe, lambda h=h: nc.tensor.transpose(out=pst[:, h * 128:(h + 1) * 128], in_=kit[:, h * 128:(h + 1) * 128], identity=C.idt[:]), [t_kit, C.t_idt], [t_pst], sig=False)
        for h in range(NH):
            P.op(P.pe, lambda h=h: nc.tensor.transpose(out=pst[:, 512 + h * 128:512 + (h + 1) * 128], in_=qdt[:, h * 128:(h + 1) * 128], identity=C.idt[:]), [t_qdt, C.t_idt], [t_pst], sig=(h == NH - 1))
        kiT, t_kiT = W.kiTp.get()
        P.op(P.act, lambda: nc.scalar.copy(out=kiT[:], in_=pst[:, 0:512]), [t_pst], [t_kiT])
        if C.dbg == "h3":
            return None
        v3 = lambda ap: ap.rearrange("p (h t) -> p h t", h=NH)
        qd, t_qd = W.qdp.get()
        for c in range(2):
            P.op(P.act, lambda c=c: nc.scalar.copy(out=qd[:, c, :, c * 64:(c + 1) * 64], in_=v3(pst[:, 512:1024])[:, :, c * 64:(c + 1) * 64]), [t_pst], [t_qd])
        yield
        ap_, t_ap = C.psA.get()
        for h in range(NH):
            for c in range(2):
                P.op(P.pe, lambda h=h, c=c: nc.tensor.matmul(ap_[:, h * 128 + c * 64:h * 128 + (c + 1) * 64], lhsT=kiT[:, h * 128:(h + 1) * 128], rhs=qd[:, c, h, c * 64:(c + 1) * 64], start=True, stop=True),
                     [t_kiT, t_qd], [t_ap], sig=(h == NH - 1 and c == 1))
        am, t_am = W.amp.get()
        P.op(P.dve, lambda: nc.vector.tensor_tensor(out=am[:], in0=ap_[:], in1=W.mask4[direction][:], op=ALU.mult), [t_ap, W.t_mask], [t_am])
        S.update(kit=kit, t_kit=t_kit, vb=vb, t_vb=t_vb, qd=qd, t_qd=t_qd, am=am, t_am=t_am)
        yield

    def stage2(S, it):
        tl = S['tl']; rows = S['rows']; xt = S['xt']; t_xt = S['t_xt']; xnT = S['xnT']; t_xnT = S['t_xnT']; rdx = S['rdx']
        kit = S['kit']; t_kit = S['t_kit']; vb = S['vb']; t_vb = S['t_vb']; qd = S['qd']; t_qd = S['t_qd']; am = S['am']; t_am = S['t_am']; dec = S['dec']; t_dec = S['t_dec']
        dprev, t_dprev = state['dprev']
        if it > 0 and (tl % C.seg_tiles == (0 if fwd else C.seg_tiles - 1)):
            P.op(P.dve, lambda: nc.vector.tensor_scalar(out=dprev[:], in0=dprev[:], scalar1=W.cf[:, 0:1], scalar2=None, op0=ALU.mult), [t_dprev, W.t_cf], [t_dprev])
        ops = []
        for hp in range(2):
            o_, t_o = C.psO.get()
            ops.append((o_, t_o))
        sbs = {}
        for ci, c in enumerate(chunks):
            for h in range(NH):
                dsc = dprev[:, h * 2 + chunks[1]:h * 2 + chunks[1] + 1] if ci == 0 else dec[:, h * 2 + chunks[0]:h * 2 + chunks[0] + 1]
                t_ds = t_dprev if ci == 0 else t_dec
                sb_, t_sb = W.sbp.get()
                P.op(P.dve, lambda sb_=sb_, dsc=dsc: nc.vector.tensor_scalar(out=sb_[:], in0=W.T[h][:], scalar1=dsc, scalar2=None, op0=ALU.mult), [W.t_T[h], t_ds], [t_sb])
                sbs[(h, c)] = (sb_, t_sb)
            kvs = []
            for h in range(NH):
                kv, t_kv = C.psA.get()
                P.op(P.pe, lambda c=c, kv=kv: nc.tensor.matmul(kv[:, 0:256], lhsT=kit[c * 64:(c + 1) * 64, h * 128:(h + 1) * 128], rhs=vb[c * 64:(c + 1) * 64, h * 256:(h + 1) * 256], start=True, stop=True),
                     [t_kit, t_vb], [t_kv])
                kvs.append((kv, t_kv))
            for h in range(NH):
                dsc = dprev[:, h * 2 + chunks[1]:h * 2 + chunks[1] + 1] if ci == 0 else dec[:, h * 2 + chunks[0]:h * 2 + chunks[0] + 1]
                t_ds = t_dprev if ci == 0 else t_dec
                kv, t_kv = kvs[h]
                P.op(P.dve, lambda dsc=dsc, kv=kv: nc.vector.scalar_tensor_tensor(out=W.T[h][:], in0=W.T[h][:], scalar=dsc, in1=kv[:, 0:256], op0=ALU.mult, op1=ALU.add), [W.t_T[h], t_ds, t_kv], [W.t_T[h]])
            yield
        for h in range(NH):
            o_, t_o = ops[h // 2]
            osl = o_[:, (h % 2) * 256:(h % 2 + 1) * 256]
            P.op(P.pe, lambda: nc.tensor.matmul(osl, lhsT=am[:, h * 128:(h + 1) * 128], rhs=vb[:, h * 256:(h + 1) * 256], start=True, stop=False), [t_am, t_vb], [t_o], sig=False)
            for k_, c in enumerate(chunks):
                sb_, t_sb = sbs[(h, c)]
                P.op(P.pe, lambda c=c, sb_=sb_: nc.tensor.matmul(osl, lhsT=qd[:, c, h, :], rhs=sb_[:], start=False, stop=(k_ == 1)), [t_qd, t_sb], [t_o], sig=(k_ == 1))
            if h % 2 == 1:
                yield
        state['dprev'] = (dec, t_dec)
        yield
        if fwd:
            of, t_of = W.ofp.get()
            for hp in range(2):
                P.op(P.act, lambda hp=hp: nc.scalar.copy(out=of[:, hp * 512:(hp + 1) * 512], in_=ops[hp][0][:]), [ops[hp][1]], [t_of])
            P.dma(P.pool, of_dram[rows, :], of[:], [t_of], [C.t_of], t_of)
            return
        of, t_of = W.ofp.get()
        P.dma(P.sp, of[:], of_dram[rows, :], [C.t_of], [t_of], t_of)
        for hp in range(2):
            P.op(P.dve, lambda hp=hp: nc.vector.tensor_tensor(out=of[:, hp * 512:(hp + 1) * 512], in0=ops[hp][0][:], in1=of[:, hp * 512:(hp + 1) * 512], op=ALU.add), [ops[hp][1], t_of], [t_of])
        S['of'] = of
        S['t_of'] = t_of
        yield

    def stage2b(S):
        rows = S['rows']; xt = S['xt']; t_xt = S['t_xt']; xnT = S['xnT']; t_xnT = S['t_xnT']; rdx = S['rdx']; of = S['of']; t_of = S['t_of']
        junk, t_junk = C.junk.get()
        st, t_st = W.small.get()
        for h in range(NH):
            P.op(P.act, lambda h=h: nc.scalar.activation(out=junk[:, h * 256:(h + 1) * 256], in_=of[:, h * 256:(h + 1) * 256], func=AF.Square, scale=1.0 / 16, accum_out=st[:, h * 8:h * 8 + 1]), [t_of], [t_junk, t_st])
        s4 = lambda o: st[:].rearrange("p (h e) -> p h e", e=8)[:, 0:4, o]
        P.op(P.act, lambda: nc.scalar.activation(out=s4(1), in_=s4(0), func=AF.Ln, bias=EPS), [t_st], [t_st])
        P.op(P.act, lambda: nc.scalar.activation(out=s4(2), in_=s4(1), func=AF.Exp, scale=-0.5), [t_st], [t_st])
        on, t_on = W.onp.get()
        for h in range(NH):
            P.op(P.dve, lambda h=h: nc.vector.scalar_tensor_tensor(out=on[:, h * 256:(h + 1) * 256], in0=of[:, h * 256:(h + 1) * 256], scalar=st[:, h * 8 + 2:h * 8 + 3], in1=W.gon[:, h * 256:(h + 1) * 256], op0=ALU.mult, op1=ALU.mult),
                 [t_of, t_st, W.t_gon], [t_on])
        yield
        sr, t_sr = W.srp.get()
        for j in range(2):
            rp, t_rp = C.psA.get()
            for kc in range(8):
                P.op(P.pe, lambda kc=kc: nc.tensor.matmul(rp[:], lhsT=xnT[:, kc, :], rhs=W.win[:, kc, 2048 + j * 512:2560 + j * 512], start=(kc == 0), stop=(kc == 7)), rdx, [t_rp], sig=(kc == 7))
            P.op(P.act, lambda: nc.scalar.activation(out=sr[:, j * 512:(j + 1) * 512], in_=rp[:], func=AF.Silu), [t_rp], [t_sr])
            yield
        og, t_og = W.xnp.get()
        P.op(P.dve, lambda: nc.vector.tensor_tensor(out=og[:], in0=on[:], in1=sr[:], op=ALU.mult), [t_on, t_sr], [t_og])
        ogT, t_ogT = W.xnTp.get()
        transpose8(nc, P, C, og, t_og, ogT[:], t_ogT)
        yield
        ys = []
        for dh in range(2):
            y, t_y = C.psA.get()
            for kc in range(8):
                P.op(P.pe, lambda kc=kc: nc.tensor.matmul(y[:], lhsT=ogT[:, kc, :], rhs=W.wout[:, kc, dh * 512:(dh + 1) * 512], start=(kc == 0), stop=(kc == 7)), [t_ogT, W.t_wout], [t_y], sig=(kc == 7))
            ys.append((y, t_y))
        post_residual(nc, P, C, ys[0][0], ys[0][1], ys[1][0], ys[1][1], W.g3, W.t_g3, xt, t_xt, 1.0, W.small, W.xop, dst[rows, :])
        yield

    Ss = [dict() for _ in range(NTILE)]
    run_interleaved([stage1a(order[0], Ss[0])])
    run_interleaved([stage1b(Ss[0]), stage1a(order[1], Ss[1]) if NTILE > 1 else None])
    for it in range(NTILE):
        run_interleaved([stage2(Ss[it], it),
                         stage2b(Ss[it - 1]) if (not fwd and it > 0) else None,
                         stage1b(Ss[it + 1]) if it + 1 < NTILE else None,
                         stage1a(order[it + 2], Ss[it + 2]) if it + 2 < NTILE else None])
        if it > 0:
            Ss[it - 1] = None
    if not fwd:
        run_interleaved([stage2b(Ss[NTILE - 1])])


def gla_layer(nc, P, C, src, dst, j, g2vec, g3vec):
    with ExitStack() as es:
        W = Ctx()
        W.win = sb(nc, es, "gwin", [128, 8, GLA_IN], BF16)
        W.t_win = Tok()
        for c0 in range(0, GLA_IN, 1024):
            c1 = min(GLA_IN, c0 + 1024)
            P.dma(P.sp, W.win[:, :, c0:c1], C.gwin_s[j][:, :, c0:c1], [C.t_wscr], [W.t_win], W.t_win)
        W.wout = sb(nc, es, "gwout", [128, 8, 1024], BF16)
        W.t_wout = Tok()
        P.dma(P.sp, W.wout[:], C.gwout_s[j], [C.t_wscr], [W.t_wout], W.t_wout)
        W.wg = [sb(nc, es, f"wg{d}", [32, 512], BF16) for d in range(2)]
        W.t_wg = Tok()
        with ExitStack() as es2:
            wgs = sb(nc, es2, "wgs", [32, 2, 512], F32)
            t_wgs = Tok()
            for d in range(2):
                P.dma(P.sp, wgs[0:16, d, :], C.gate_w[d][j], [], [t_wgs], t_wgs)
                P.dma(P.sp, wgs[16:17, d, :], C.gate_b[d][j:j + 1, :], [], [t_wgs], t_wgs)
            for d in range(2):
                P.op(P.dve, lambda d=d: nc.vector.tensor_copy(out=W.wg[d][0:17, :], in_=wgs[0:17, d, :]), [t_wgs], [W.t_wg])
            P.barrier([t_wgs])
        W.tri = [sb(nc, es, f"tri{d}", [128, 128], F32) for d in range(2)]
        W.t_tri = Tok()
        W.mask4 = [sb(nc, es, f"mask{d}", [128, 512], F32) for d in range(2)]
        W.t_mask = Tok()
        for d in range(2):
            P.dma(P.sp, W.tri[d][:], C.tri_in[d], [], [W.t_tri], W.t_tri)
            P.dma(P.sp, W.mask4[d][:], C.mask_in[d], [], [W.t_mask], W.t_mask)
        W.tri2 = sb(nc, es, "tri2", [128, 16], F32)
        P.dma(P.sp, W.tri2[:], C.tri2_in, [], [W.t_tri], W.t_tri)
        W.qdtp = tiles(nc, es, "qdt", [128, 512], BF16, 2)
        W.cf = sb(nc, es, "cf", [128, 1], F32)
        W.t_cf = Tok()
        P.dma(P.sp, W.cf[:], C.cf_in, [], [W.t_cf], W.t_cf)
        W.g2, W.t_g2 = load_gain(nc, P, C, es, "g2", g2vec)
        W.g3, W.t_g3 = load_gain(nc, P, C, es, "g3", g3vec)
        W.gon, W.t_gon = load_gain(nc, P, C, es, "gon", C.gla_onorm[j])
        W.T = [sb(nc, es, f"T{h}", [128, 256], F32) for h in range(NH)]
        W.t_T = [Tok() for _ in range(NH)]
        W.xtp = tiles(nc, es, "xt", [128, 1024], F32, 5)
        W.xnp = tiles(nc, es, "xn", [128, 1024], BF16, 2)
        W.xnTp = tiles(nc, es, "xnT", [128, 8, 128], BF16, 8)
        C.junk = tiles(nc, es, "junk", [128, 1024], BF16, 1)
        W.small = tiles(nc, es, "small", [128, 32], F32, 4)
        W.vbp = tiles(nc, es, "vb", [128, 1024], BF16, 3)
        W.gap = tiles(nc, es, "ga", [32, 128], BF16, 2)
        for g_ in W.gap.tiles:
            P.op(P.dve, lambda g_=g_: nc.vector.memset(g_[:], 1.0), [], [W.gap.toks[W.gap.tiles.index(g_)]])
        W.ep = tiles(nc, es, "ee", [128, 512], F32, 2)
        W.lp = tiles(nc, es, "ll", [128, 512], F32, 2)
        W.eitp = tiles(nc, es, "eit", [128, 512], F32, 3)
        W.ebTp = tiles(nc, es, "ebT", [128, 512], F32, 3)
        W.kitp = tiles(nc, es, "kit", [128, 512], BF16, 3)
        W.qdp = tiles(nc, es, "qd", [128, 2, NH, 128], BF16, 3)
        for q_ in W.qdp.tiles:
            P.op(P.dve, lambda q_=q_: nc.vector.memset(q_[:], 0.0), [], [W.qdp.toks[W.qdp.tiles.index(q_)]])
        W.kiTp = tiles(nc, es, "kiT", [128, 512], BF16, 2)
        W.decp = tiles(nc, es, "dec", [128, 8], F32, 6)
        W.amp = tiles(nc, es, "am", [128, 512], BF16, 3)
        W.sbp = tiles(nc, es, "sbs", [128, 256], BF16, 8)
        W.ofp = tiles(nc, es, "of", [128, 1024], F32, 2)
        W.onp = tiles(nc, es, "on", [128, 1024], F32, 1)
        W.srp = tiles(nc, es, "sr", [128, 1024], F32, 1)
        W.xop = tiles(nc, es, "xo", [128, 1024], F32, 2)
        C.ysp = tiles(nc, es, "ys", [128, 1024], F32, 1)
        gla_sweep(nc, P, C, es, W, 0, src, dst, C.of_dram)
        P.barrier([C.t_of, C.t_xsrc])
        gla_sweep(nc, P, C, es, W, 1, src, dst, C.of_dram)
        P.barrier([C.t_xdst, C.t_xsrc, C.t_wscr, C.t_of])


SWA_IN = 1536
NEG = -1e30


def transposeN(nc, P, C, src, t_src, n, dstT, t_dst):
    ps, t_ps = C.psT.get()
    for k in range(n):
        P.op(P.pe, lambda k=k: nc.tensor.transpose(out=ps[:, k * 128:(k + 1) * 128], in_=src[:, k * 128:(k + 1) * 128], identity=C.idt[:]),
             [t_src, C.t_idt], [t_ps], sig=(k == n - 1))
    P.op(P.act, lambda: nc.scalar.copy(out=dstT, in_=ps[:, 0:n * 128].rearrange("p (k t) -> p k t", k=n)), [t_ps], [t_dst])


def rope_inplace(nc, P, W, xs, t_xs, nheads, rt, t_rt):
    xv = xs[:, 0:nheads * 64].rearrange("p (h d) -> p h d", d=64)
    x1 = xv[:, :, 0:8]
    x2 = xv[:, :, 8:16]
    cos = rt[:, 0:nheads * 8].rearrange("p (h d) -> p h d", d=8)
    sin = rt[:, 128:128 + nheads * 8].rearrange("p (h d) -> p h d", d=8)
    tp, t_tp = W.ropet.get()
    tv = lambda k: tp[:, k, 0:nheads * 8].rearrange("p (h d) -> p h d", d=8)
    P.op(P.dve, lambda: nc.vector.tensor_tensor(out=tv(0), in0=x1, in1=cos, op=ALU.mult), [t_xs, t_rt], [t_tp])
    P.op(P.dve, lambda: nc.vector.tensor_tensor(out=tv(1), in0=x2, in1=sin, op=ALU.mult), [t_xs, t_rt], [t_tp])
    P.op(P.dve, lambda: nc.vector.tensor_tensor(out=tv(2), in0=x2, in1=cos, op=ALU.mult), [t_xs, t_rt], [t_tp])
    P.op(P.dve, lambda: nc.vector.tensor_tensor(out=tv(3), in0=x1, in1=sin, op=ALU.mult), [t_xs, t_rt], [t_tp])
    P.op(P.dve, lambda: nc.vector.tensor_tensor(out=x1, in0=tv(0), in1=tv(1), op=ALU.subtract), [t_tp], [t_xs])
    P.op(P.dve, lambda: nc.vector.tensor_tensor(out=x2, in0=tv(2), in1=tv(3), op=ALU.add), [t_tp], [t_xs])


def swa_layer(nc, P, C, src, dst, j, g2vec, g3vec):
    NTILE = C.NTOK // 128
    ST = C.seg_tiles
    with ExitStack() as es:
        W = Ctx()
        W.win = sb(nc, es, "swin", [128, 8, SWA_IN], BF16)
        W.t_win = Tok()
        P.dma(P.sp, W.win[:], C.swin_s[j], [C.t_wscr], [W.t_win], W.t_win)
        W.wout = sb(nc, es, "swout", [128, 8, 1024], BF16)
        W.t_wout = Tok()
        P.dma(P.sp, W.wout[:], C.swout_s[j], [C.t_wscr], [W.t_wout], W.t_wout)
        masks = sb(nc, es, "smask", [128, 3, 4, 384], F32)
        t_masks = Tok()
        P.dma(P.sp, masks[:], C.swa_mask_in, [], [t_masks], t_masks)
        sink = sb(nc, es, "sink", [128, 16], F32)
        t_sink = Tok()
        P.dma(P.sp, sink[:], C.swa_sinks[j].partition_broadcast(128), [], [t_sink], t_sink)
        W.g2, W.t_g2 = load_gain(nc, P, C, es, "g2", g2vec)
        W.g3, W.t_g3 = load_gain(nc, P, C, es, "g3", g3vec)
        xtp = tiles(nc, es, "xt", [128, 1024], F32, 5)
        xnp = tiles(nc, es, "xn", [128, 1024], BF16, 5)
        xnTp = tiles(nc, es, "xnT", [128, 8, 128], BF16, 3)
        C.junk = tiles(nc, es, "junk", [128, 1024], BF16, 1)
        small = tiles(nc, es, "small", [128, 32], F32, 4)
        qsp = tiles(nc, es, "qs", [128, 1024], F32, 2)
        ksp = tiles(nc, es, "ks", [128, 256], F32, 2)
        rtp = tiles(nc, es, "rt", [128, 256], F32, 2)
        W.ropet = tiles(nc, es, "ropet", [128, 4, 128], F32, 2)
        kb2p = tiles(nc, es, "kb2", [128, 4, 2, 64], BF16, 2)
        qTp = tiles(nc, es, "qT", [128, 8, 128], BF16, 5)
        kT2p = tiles(nc, es, "kT2", [128, 4, 128], BF16, 6)
        vbp = tiles(nc, es, "vb", [128, 256], BF16, 6)
        smp = tiles(nc, es, "sm", [128, 4, 392], F32, 4)
        pbp = tiles(nc, es, "pb", [128, 4, 392], BF16, 4)
        pTp = tiles(nc, es, "pT", [128, 4, 384], BF16, 4)
        stp = tiles(nc, es, "sst", [128, 16], F32, 5)
        xop = tiles(nc, es, "xo", [128, 1024], F32, 2)
        C.ysp = tiles(nc, es, "ys", [128, 1024], F32, 2)
        st8 = {}
        epi = {}

        def inproj(tl):
            rows = slice(tl * 128, (tl + 1) * 128)
            xt, t_xt = xtp.get()
            P.dma(P.sp, xt[:], src[rows, :], [C.t_xsrc], [t_xt], t_xt)
            rt, t_rt = rtp.get()
            P.dma(P.sp, rt[:], C.rope_in[rows, :], [], [t_rt], t_rt)
            xnT, t_xnT = xnTp.get()
            norm_transpose(nc, P, C, xt, t_xt, W.g2, W.t_g2, xnT[:], t_xnT, small, xnp)
            yield
            rdx = [t_xnT, W.t_win]
            qs, t_qs = qsp.get()
            for jh in range(2):
                qp, t_qp = C.psA.get()
                for kc in range(8):
                    P.op(P.pe, lambda kc=kc: nc.tensor.matmul(qp[:], lhsT=xnT[:, kc, :], rhs=W.win[:, kc, jh * 512:(jh + 1) * 512], start=(kc == 0), stop=(kc == 7)), rdx, [t_qp], sig=(kc == 7))
                P.op(P.act, lambda: nc.scalar.activation(out=qs[:, jh * 512:(jh + 1) * 512], in_=qp[:], func=AF.Copy, scale=0.125), [t_qp], [t_qs])
                yield
            kvp, t_kvp = C.psA.get()
            for kc in range(8):
                P.op(P.pe, lambda kc=kc: nc.tensor.matmul(kvp[:], lhsT=xnT[:, kc, :], rhs=W.win[:, kc, 1024:1536], start=(kc == 0), stop=(kc == 7)), rdx, [t_kvp], sig=(kc == 7))
            ks, t_ks = ksp.get()
            P.op(P.act, lambda: nc.scalar.copy(out=ks[:], in_=kvp[:, 0:256]), [t_kvp], [t_ks])
            vb, t_vb = vbp.get()
            P.op(P.act, lambda: nc.scalar.copy(out=vb[:], in_=kvp[:, 256:512]), [t_kvp], [t_vb])
            yield
            rope_inplace(nc, P, W, qs, t_qs, 16, rt, t_rt)
            yield
            rope_inplace(nc, P, W, ks, t_ks, 4, rt, t_rt)
            yield
            qb, t_qb = xnp.get()
            P.op(P.act, lambda: nc.scalar.copy(out=qb[:], in_=qs[:]), [t_qs], [t_qb])
            qT, t_qT = qTp.get()
            transpose8(nc, P, C, qb, t_qb, qT[:], t_qT)
            yield
            kb2, t_kb2 = kb2p.get()
            for d_ in range(2):
                P.op(P.dve, lambda d_=d_: nc.vector.tensor_copy(out=kb2[:, :, d_, :], in_=ks[:].rearrange("p (g d) -> p g d", d=64)), [t_ks], [t_kb2])
            kT2, t_kT2 = kT2p.get()
            transposeN(nc, P, C, kb2[:].rearrange("p g d e -> p (g d e)"), t_kb2, 4, kT2[:], t_kT2)
            st8[tl] = dict(xt=(xt, t_xt), qT=(qT, t_qT), kT2=(kT2, t_kT2), vb=(vb, t_vb))
            yield

        def group(tl, g, keys, msk, ob, on, t_on):
            nk = len(keys)
            Wd = nk * 128
            qT, t_qT = st8[tl]["qT"]
            C.psA.i = 0
            banks = [C.psA.get() for _ in range(4)]
            t_banks = [b[1] for b in banks]
            for hh in range(4):
                hq = g * 4 + hh
                p_, half = hq // 2, hq % 2
                pr = slice(half * 64, (half + 1) * 64)
                for jj, kt_ in enumerate(keys):
                    kT2, t_kT2 = st8[kt_]["kT2"]
                    P.op(P.pe, lambda jj=jj, kT2=kT2: nc.tensor.matmul(C.psS[:, hh, jj * 128:(jj + 1) * 128], lhsT=qT[pr, p_, :], rhs=kT2[pr, g, :], start=True, stop=True),
                         [t_qT, t_kT2], [t_banks[hh]], sig=(jj == nk - 1))
            sm, t_sm = smp.get()
            P.op(P.dve, lambda: nc.vector.tensor_tensor(out=sm[:, :, 0:Wd], in0=C.psS[:, :, 0:Wd], in1=msk, op=ALU.add), t_banks + [t_masks], [t_sm])
            P.op(P.act, lambda: nc.scalar.copy(out=sm[:, :, Wd], in_=sink[:, g * 4:(g + 1) * 4]), [t_sink], [t_sm])
            yield
            st, t_st = stp.get()
            P.op(P.dve, lambda: nc.vector.reduce_max(out=st[:, 0:4], in_=sm[:, :, 0:Wd + 1], axis=AX.X, negate=True), [t_sm], [t_st])
            yield
            pb, t_pb = pbp.get()
            for hh in range(4):
                P.op(P.act, lambda hh=hh: nc.scalar.activation(out=pb[:, hh, 0:Wd + 1], in_=sm[:, hh, 0:Wd + 1], func=AF.Exp, bias=st[:, hh:hh + 1], accum_out=st[:, 8 + hh:9 + hh]), [t_sm, t_st], [t_pb, t_st])
                if hh % 2 == 1:
                    yield
            P.op(P.dve, lambda: nc.vector.reciprocal(out=st[:, 4:8], in_=st[:, 8:12]), [t_st], [t_st])
            pT, t_pT = pTp.get()
            for h2 in range(2):
                ps, t_ps = C.psT.get()
                for hh in range(2):
                    for jj in range(nk):
                        P.op(P.pe, lambda hh=hh, jj=jj: nc.tensor.transpose(out=ps[:, (hh * nk + jj) * 128:(hh * nk + jj + 1) * 128], in_=pb[:, h2 * 2 + hh, jj * 128:(jj + 1) * 128], identity=C.idt[:]),
                             [t_pb, C.t_idt], [t_ps], sig=(hh == 1 and jj == nk - 1))
                P.op(P.act if h2 == 0 else P.dve, (lambda: nc.scalar.copy(out=pT[:, h2 * 2:h2 * 2 + 2, 0:Wd], in_=ps[:, 0:2 * Wd].rearrange("p (h w) -> p h w", h=2))) if h2 == 0 else
                     (lambda: nc.vector.tensor_copy(out=pT[:, h2 * 2:h2 * 2 + 2, 0:Wd], in_=ps[:, 0:2 * Wd].rearrange("p (h w) -> p h w", h=2))), [t_ps], [t_pT])
                yield
            o_, t_o = ob[g // 2]
            for hh in range(4):
                hq = g * 4 + hh
                osl = o_[:, (hq % 8) * 64:(hq % 8 + 1) * 64]
                for jj, kt_ in enumerate(keys):
                    vb, t_vb = st8[kt_]["vb"]
                    P.op(P.pe, lambda jj=jj, vb=vb: nc.tensor.matmul(osl, lhsT=pT[:, hh, jj * 128:(jj + 1) * 128], rhs=vb[:, g * 64:(g + 1) * 64], start=(jj == 0), stop=(jj == nk - 1)),
                         [t_pT, t_vb], [t_o], sig=(jj == nk - 1 and hh == 3))
            gsl = slice((g % 2) * 256, (g % 2 + 1) * 256)
            P.op(P.dve, lambda: nc.vector.tensor_tensor(out=on[:, g * 256:(g + 1) * 256].rearrange("p (h d) -> p h d", h=4), in0=o_[:, gsl].rearrange("p (h d) -> p h d", h=4),
                                                        in1=st[:, 4:8].unsqueeze(2).broadcast_to([128, 4, 64]), op=ALU.mult), [t_o, t_st], [t_on])
            yield

        def attn(tl):
            rows = slice(tl * 128, (tl + 1) * 128)
            S = st8[tl]
            keys = [t for t in (tl - 1, tl, tl + 1) if 0 <= t < NTILE]
            nk = len(keys)
            Wd = nk * 128
            mk = 0
            if tl % ST == 0 and tl > 0:
                mk = 1
            if tl % ST == ST - 1 and tl < NTILE - 1:
                mk = 2
            c0 = 0 if tl > 0 else 128
            msk = masks[:, mk, :, c0:c0 + Wd]
            ob = [C.psO.get() for _ in range(2)]
            on, t_on = xnp.get()
            ggens = [group(tl, g, keys, msk, ob, on, t_on) for g in range(4)]
            while ggens:
                for gg in list(ggens):
                    try:
                        next(gg)
                    except StopIteration:
                        ggens.remove(gg)
                    yield
            epi[tl] = (on, t_on)

        def epilogue(tl):
            rows = slice(tl * 128, (tl + 1) * 128)
            on, t_on = epi.pop(tl)
            onT, t_onT = xnTp.get()
            transpose8(nc, P, C, on, t_on, onT[:], t_onT)
            yield
            ys = []
            for dh in range(2):
                y, t_y = C.psA.get()
                for kc in range(8):
                    P.op(P.pe, lambda kc=kc: nc.tensor.matmul(y[:], lhsT=onT[:, kc, :], rhs=W.wout[:, kc, dh * 512:(dh + 1) * 512], start=(kc == 0), stop=(kc == 7)), [t_onT, W.t_wout], [t_y], sig=(kc == 7))
                ys.append((y, t_y))
            xt, t_xt = st8[tl]["xt"]
            post_residual(nc, P, C, ys[0][0], ys[0][1], ys[1][0], ys[1][1], W.g3, W.t_g3, xt, t_xt, 1.0, small, xop, dst[rows, :])
            st8.pop(tl - 1, None)
            yield

        run_interleaved([inproj(0)])
        for k_ in (1, 2):
            if NTILE > k_:
                run_interleaved([inproj(k_)])
        for tl in range(NTILE):
            run_interleaved([attn(tl), epilogue(tl - 1) if tl > 0 else None, inproj(tl + 3) if tl + 3 < NTILE else None])
        run_interleaved([epilogue(NTILE - 1)])
        P.barrier([C.t_xdst, C.t_xsrc, C.t_wscr])


def w1_jobs(w1, w1s):
    v = w1.rearrange("(kc p) f -> p kc f", p=128)
    jobs = []
    for fp in range(NFC):
        srcs = [((0, 128), v[:, :, fp * 128:(fp + 1) * 128]), ((128, 256), v[:, :, DFF + fp * 128:DFF + (fp + 1) * 128])]
        jobs.append((srcs, w1s[fp], "gen"))
    return jobs


def build(cfg):
    nc = bass.Bass("TRN2", target_bir_lowering=False)
    P = Prog(nc)
    C = Ctx()
    NTOK = cfg["NTOK"]
    C.NTOK = NTOK
    C.dbg = cfg.get("dbg")
    din = lambda name, shape, dt=F32: nc.dram_tensor(name, shape, dt, kind="ExternalInput").ap()
    dscr = lambda name, shape, dt=F32: nc.dram_tensor(name, shape, dt, kind="Internal").ap()
    x_in = din("x", [NTOK, D])
    norm_g = din("norm_g", [4, 6, D])
    NL = cfg.get("NL", 4)
    ffn_w1 = din("ffn_w1", [NL, 2, D, 2 * DFF])
    ffn_w2 = din("ffn_w2", [NL, 2, DFF, D])
    C.ident = din("ident", [128, 128], BF16)
    NG = cfg.get("NG", 2)
    gla_w_in = din("gla_w_in", [NG, D, GLA_IN])
    C.gate_w = [din("gla_w_gate_f", [NG, 16, 512]), din("gla_w_gate_b", [NG, 16, 512])]
    C.gate_b = [din("gla_b_gate_f", [NG, 512]), din("gla_b_gate_b", [NG, 512])]
    C.gla_onorm = din("gla_onorm", [NG, D])
    gla_w_out = din("gla_w_out", [NG, D, D])
    C.tri_in = [din("tri_f", [128, 128]), din("tri_b", [128, 128])]
    C.mask_in = [din("mask_f", [128, 512]), din("mask_b", [128, 512])]
    C.cf_in = din("cf", [128, 1])
    C.tri2_in = din("tri2", [128, 16])
    C.seg_tiles = cfg.get("seg_tiles", 16)
    C.of_dram = dscr("of_dram", [NTOK, D])
    NS = cfg.get("NS", 2)
    swa_w_in = din("swa_w_in", [NS, D, SWA_IN])
    C.swa_sinks = din("swa_sinks", [NS, 16])
    swa_w_out = din("swa_w_out", [NS, D, D])
    C.swa_mask_in = din("swa_mask", [128, 3, 4, 384])
    C.rope_in = din("rope", [NTOK, 256])
    C.swin_s = {}
    C.swout_s = {}
    C.t_of = Tok()
    C.gwin_s = {}
    C.gwout_s = {}
    y = nc.dram_tensor("y", [NTOK, D], F32, kind="ExternalOutput").ap()
    xs = [dscr("xs0", [NTOK, D]), dscr("xs1", [NTOK, D])]
    C.t_wscr = Tok()
    C.t_xsrc = Tok()
    C.t_xdst = Tok()
    setup_globals(nc, P, C)
    phases = cfg["phases"]
    w1s = {}
    w2s = {}
    pjobs = []
    wtok = [Tok() for _ in phases]
    for pi, ph in enumerate(phases):
        jobs = []
        if ph[0] == "ffn":
            _, li, k = ph
            w1s[(li, k)] = dscr(f"w1s_{li}_{k}", [NFC, 128, 8, 256], BF16)
            w2s[(li, k)] = dscr(f"w2s_{li}_{k}", [128, NFC, 1024], BF16)
            jobs += w1_jobs(ffn_w1[li, k], w1s[(li, k)])
            v2 = ffn_w2[li, k].rearrange("(fc p) d -> p fc d", p=128)
            for fc in range(0, NFC, 2):
                jobs.append(([((0, 1024), v2[:, fc:fc + 2, :])], w2s[(li, k)][:, fc:fc + 2, :], "gen"))
        if ph[0] == "gla":
            j = ph[1]
            C.gwin_s[j] = dscr(f"gwin_s{j}", [128, 8, GLA_IN], BF16)
            C.gwout_s[j] = dscr(f"gwout_s{j}", [128, 8, D], BF16)
            jobs += mat_jobs(gla_w_in[j], C.gwin_s[j], GLA_IN)
            jobs += mat_jobs(gla_w_out[j], C.gwout_s[j], D)
        if ph[0] == "swa":
            j = ph[1]
            C.swin_s[j] = dscr(f"swin_s{j}", [128, 8, SWA_IN], BF16)
            C.swout_s[j] = dscr(f"swout_s{j}", [128, 8, D], BF16)
            jobs += mat_jobs(swa_w_in[j], C.swin_s[j], SWA_IN)
            jobs += mat_jobs(swa_w_out[j], C.swout_s[j], D)
        pjobs.append([(srcs, dst_, wtok[pi]) for (srcs, dst_, _k) in jobs])
    ffn_idx = [i for i, ph in enumerate(phases) if ph[0] == "ffn"]
    bg_for = {}
    if cfg.get("bg_conv", True) and ffn_idx:
        first = ffn_idx[0]
        upfront = [j for pj in pjobs[:first + 1] for j in pj]
        for a_, b_ in zip(ffn_idx, ffn_idx[1:] + [len(phases) - 1]):
            bg_for[a_] = [j for pj in pjobs[a_ + 1:b_ + 1] for j in pj]
        if ffn_idx[-1] < len(phases) - 1:
            bg_for[ffn_idx[-1]] = [j for pj in pjobs[ffn_idx[-1] + 1:] for j in pj]
    else:
        upfront = [j for pj in pjobs for j in pj]
    convert_weights(nc, P, C, upfront)
    cur = x_in
    for i, ph in enumerate(phases):
        dst = y if i == len(phases) - 1 else xs[i % 2]
        C.t_wscr = wtok[i]
        if ph[0] == "ffn":
            _, li, k = ph
            ffn_phase(nc, P, C, cur, dst, w1s[(li, k)], w2s[(li, k)], norm_g[li, 0 if k == 0 else 4], norm_g[li, 1 if k == 0 else 5], bg_jobs=bg_for.get(i, ()))
        if ph[0] == "gla":
            _, j, li = ph
            gla_layer(nc, P, C, cur, dst, j, norm_g[li, 2], norm_g[li, 3])
        if ph[0] == "swa":
            _, j, li = ph
            swa_layer(nc, P, C, cur, dst, j, norm_g[li, 2], norm_g[li, 3])
        cur = dst
    P.barrier([C.t_xdst])
    print("deadlock check:", simulate(P)[0])
    print("instructions:", P.nins, "sems:", [e.nsem for e in P.engs], len(P.dma_toks))
    return nc


NCORES = 8
SEQ = 8192
DEC_SEQ = 2048
SEG_TILES = DEC_SEQ // 128


def _consts(chain):
    s_ = np.arange(128)[:, None]
    c_ = np.arange(128)[None, :]
    same = (s_ // 64) == (c_ // 64)
    mf = (same & (s_ <= c_)).astype(np.float32)
    mb = (same & (s_ >= c_)).astype(np.float32)
    mp = np.where(c_ >= s_, 0.0, NEG).astype(np.float32)
    mn = np.where(c_ <= s_, 0.0, NEG).astype(np.float32)
    z = np.zeros((128, 128), np.float32)
    full = np.full((128, 128), NEG, np.float32)
    bp = mp if chain else full
    bn = mn if chain else full
    swa_mask = np.stack([np.concatenate([mp, z, mn], 1), np.concatenate([bp, z, mn], 1), np.concatenate([mp, z, bn], 1)], 1)
    swa_mask = np.repeat(swa_mask[:, :, None, :], 4, axis=2)
    tri2 = np.zeros((128, 16), np.float32)
    tri2[0:64, 0] = -1.0 / 16
    tri2[64:128, 1] = -1.0 / 16
    return dict(ident=np.eye(128).astype(ml_dtypes.bfloat16), tri_f=-mf / 16.0, tri_b=-mb / 16.0, tri2=tri2,
                mask_f=np.tile(mf, (1, 4)), mask_b=np.tile(mb, (1, 4)), swa_mask=np.ascontiguousarray(swa_mask),
                cf=np.full((128, 1), 1.0 if chain else 0.0, np.float32))


def _rope_table(pos):
    inv = (500000.0 ** (-(np.arange(8, dtype=np.float32) * 2.0 / 16))).astype(np.float32)
    ang = pos.astype(np.float32)[:, None] * inv[None, :]
    cos = np.cos(ang).astype(np.float32)
    sin = np.sin(ang).astype(np.float32)
    return np.ascontiguousarray(np.concatenate([np.tile(cos, (1, 16)), np.tile(sin, (1, 16))], 1).astype(np.float32))


def full_phases():
    ph = []
    for li in range(4):
        ph.append(("ffn", li, 0))
        ph.append(("gla", li // 2, li) if li % 2 == 0 else ("swa", li // 2, li))
        ph.append(("ffn", li, 1))
    return ph


def kernel(**inputs):
    f32 = lambda a: np.ascontiguousarray(np.asarray(a, dtype=np.float32))
    xp = f32(inputs["x_prompt"])
    xsm = f32(inputs["x_sample"])
    wnames = ["norm_g", "ffn_w1", "ffn_w2", "gla_w_in", "gla_w_gate_f", "gla_b_gate_f", "gla_w_gate_b", "gla_b_gate_b",
              "gla_onorm", "gla_w_out", "swa_w_in", "swa_sinks", "swa_w_out"]
    wts = {k: f32(inputs[k]) for k in wnames}
    nc = build(dict(NTOK=SEQ, NL=4, NG=2, NS=2, seg_tiles=SEG_TILES, phases=full_phases()))
    cP = _consts(True)
    cS = _consts(False)
    ropeP = _rope_table(np.arange(SEQ))
    ropeS = _rope_table(np.tile(np.arange(DEC_SEQ), SEQ // DEC_SEQ))
    in_maps = []
    for c in range(NCORES):
        m = dict(wts)
        if c < 4:
            m["x"] = xp[c]
            m["rope"] = ropeP
            m.update(cP)
        else:
            k = c - 4
            x = np.zeros((SEQ, D), np.float32)
            x[0:DEC_SEQ] = xsm[2 * k]
            x[DEC_SEQ:2 * DEC_SEQ] = xsm[2 * k + 1]
            m["x"] = x
            m["rope"] = ropeS
            m.update(cS)
        in_maps.append(m)
    res = run_bass_kernel_spmd(nc, in_maps, core_ids=list(range(NCORES)))
    yp = np.stack([np.asarray(res.results[c]["y"], dtype=np.float32) for c in range(4)], 0)
    ys = np.zeros_like(xsm)
    for k in range(4):
        y = np.asarray(res.results[4 + k]["y"], dtype=np.float32)
        ys[2 * k] = y[0:DEC_SEQ]
        ys[2 * k + 1] = y[DEC_SEQ:2 * DEC_SEQ]
    return (yp, ys)
```

```python
import numpy as np
import ml_dtypes
from contextlib import ExitStack
import concourse.bass as bass
import concourse.mybir as mybir
from concourse.bass_utils import run_bass_kernel_spmd

F32 = mybir.dt.float32
BF16 = mybir.dt.bfloat16
ALU = mybir.AluOpType
AF = mybir.ActivationFunctionType
AX = mybir.AxisListType

D = 1024
DFF = 2816
NFC = DFF // 128
EPS = 1e-6
SEM_ROLL = 30000


class Dep:
    __slots__ = ("eng", "sem", "val")

    def __init__(self, eng):
        self.eng = eng
        self.sem = None
        self.val = 0


class Tok:
    __slots__ = ("w", "r", "sems")

    def __init__(self):
        self.w = None
        self.r = []
        self.sems = {}


class Eng:
    def __init__(self, prog, name, h, selfsafe=False):
        self.prog = prog
        self.name = name
        self.h = h
        self.selfsafe = selfsafe
        self.sem = None
        self.semv = 0
        self.waited = {}
        self.pending = []
        self.nsem = 0
        self.last = None

    def newsem(self):
        self.sem = self.prog.nc.alloc_semaphore(f"e_{self.name}_{self.nsem}")
        self.nsem += 1
        self.semv = 0


class Prog:
    def __init__(self, nc):
        self.nc = nc
        self.pe = Eng(self, "pe", nc.tensor, selfsafe=True)
        self.act = Eng(self, "act", nc.scalar)
        self.dve = Eng(self, "dve", nc.vector)
        self.pool = Eng(self, "pool", nc.gpsimd)
        self.sp = Eng(self, "sp", nc.sync)
        self.engs = [self.pe, self.act, self.dve, self.pool, self.sp]
        for e in self.engs:
            e.newsem()
        self.dma_toks = []
        self.free_sems = {"hw": [], "sw": []}
        self.nsem_alloc = 0
        self.nins = 0
        self.log = {e.name: [] for e in self.engs}

    def _wait_deps(self, eng, deps):
        need = {}
        for d in deps:
            if d is None:
                continue
            if d.eng is eng and eng.selfsafe:
                continue
            assert d.sem is not None, "dependency on an unsignalled instruction"
            key = d.sem
            if eng.waited.get(key.num, 0) >= d.val:
                continue
            if need.get(key.num, (None, 0))[1] < d.val:
                need[key.num] = (key, d.val)
        for num, (sem, val) in need.items():
            eng.h.wait_ge(sem, val)
            eng.waited[num] = val
            self.log[eng.name].append(("w", num, val))
            self.nins += 1

    def _collect(self, reads, writes):
        deps = []
        for t in reads:
            if t.w is not None:
                deps.append(t.w)
        for t in writes:
            if t.w is not None:
                deps.append(t.w)
            deps.extend(t.r)
        return deps

    def _record(self, dep, reads, writes):
        for t in reads:
            t.r.append(dep)
        for t in writes:
            t.w = dep
            t.r = []

    def op(self, eng, fn, reads=(), writes=(), sig=True):
        self._wait_deps(eng, self._collect(reads, writes))
        ins = fn()
        self.nins += 1
        dep = Dep(eng)
        if sig:
            if eng.semv >= SEM_ROLL:
                eng.newsem()
            eng.semv += 1
            ins.then_inc(eng.sem, 1)
            self.log[eng.name].append(("i", eng.sem.num, 1))
            dep.sem = eng.sem
            dep.val = eng.semv
            for p in eng.pending:
                p.sem = dep.sem
                p.val = dep.val
            eng.pending = []
            eng.last = dep
        else:
            eng.pending.append(dep)
        self._record(dep, reads, writes)
        return dep

    def dma(self, q, out, in_, reads, writes, stok):
        self._wait_deps(q, self._collect(reads, writes))
        sw = q is self.pool
        key = "sw" if sw else "hw"
        cur = stok.sems.get(key)
        if cur is None or cur[1] >= SEM_ROLL:
            fl = self.free_sems[key]
            while fl and fl[-1][1] >= SEM_ROLL:
                fl.pop()
            if fl:
                cur = list(fl.pop())
            else:
                self.nsem_alloc += 1
                cur = [self.nc.alloc_semaphore(f"d{key}_{self.nsem_alloc}"), 0]
            stok.sems[key] = cur
            if stok not in self.dma_toks:
                self.dma_toks.append(stok)
        cur[1] += 16
        q.h.dma_start(out=out, in_=in_).then_inc(cur[0], 16)
        self.log[q.name].append(("i", cur[0].num, 16))
        self.nins += 1
        dep = Dep(None)
        dep.sem = cur[0]
        dep.val = cur[1]
        self._record(dep, reads, writes)
        return dep

    def barrier(self, toks=()):
        deps = [e.last for e in self.engs if e.last is not None]
        for t in list(self.dma_toks) + list(toks):
            if t.w is not None:
                deps.append(t.w)
            deps.extend(t.r)
        for e in self.engs:
            assert not e.pending
            self._wait_deps(e, deps)
        for t in list(self.dma_toks) + list(toks):
            t.w = None
            t.r = []
        for t in self.dma_toks:
            for key, cur in t.sems.items():
                self.free_sems[key].append((cur[0], cur[1]))
            t.sems = {}
        self.dma_toks = []


def simulate(P):
    pos = {k: 0 for k in P.log}
    sems = {}
    prog = True
    while prog:
        prog = False
        for k, lst in P.log.items():
            while pos[k] < len(lst):
                kind, num, val = lst[pos[k]]
                if kind == "w":
                    if sems.get(num, 0) < val:
                        break
                else:
                    sems[num] = sems.get(num, 0) + val
                pos[k] += 1
                prog = True
    stuck = {k: (pos[k], len(l), l[pos[k]]) for k, l in P.log.items() if pos[k] < len(l)}
    return stuck, sems


def run_interleaved(gens):
    gens = [g for g in gens if g is not None]
    while gens:
        for g in list(gens):
            try:
                next(g)
            except StopIteration:
                gens.remove(g)


class RPool:
    def __init__(self, tiles):
        self.tiles = tiles
        self.toks = [Tok() for _ in tiles]
        self.i = 0

    def get(self):
        i = self.i
        self.i = (i + 1) % len(self.tiles)
        return self.tiles[i], self.toks[i]


class Ctx:
    pass


_uid = [0]


def sb(nc, es, name, shape, dtype):
    _uid[0] += 1
    return es.enter_context(nc.sbuf_tensor(f"{name}_{_uid[0]}", shape, dtype))


def tiles(nc, es, name, shape, dtype, n):
    return RPool([sb(nc, es, f"{name}{i}", shape, dtype) for i in range(n)])


def setup_globals(nc, P, C):
    C.psT = RPool([nc.alloc_psum_tensor(f"psT{i}", [128, 1024], BF16) for i in range(2)])
    C.psS = nc.alloc_psum_tensor("psS", [128, 4, 512], F32)
    C.psA = RPool([C.psS[:, i, :] for i in range(4)])
    C.psO = RPool([nc.alloc_psum_tensor(f"psO{i}", [128, 512], F32) for i in range(2)])
    C.idt = nc.alloc_sbuf_tensor("idt", [128, 128], BF16)
    C.t_idt = Tok()
    P.dma(P.sp, C.idt[:], C.ident, [], [C.t_idt], C.t_idt)


def conv_stream(nc, P, C, jobs, st, cb, cast_engs):
    k = 0
    for srcs, dst, tok in jobs:
        stt, t_st = st.get()
        cbt, t_cb = cb.get()
        a, b = dst.shape[1], dst.shape[2]
        assert a * b <= 2048
        sv = stt[:, 0:a * b].rearrange("p (a b) -> p a b", a=a)
        cv = cbt[:, 0:a * b].rearrange("p (a b) -> p a b", a=a)
        for (lo, hi), src in srcs:
            P.dma(P.sp, sv[:, :, lo:hi], src, [], [t_st], t_st)
        yield
        e = cast_engs[k % len(cast_engs)]
        k += 1
        if e is P.act:
            P.op(P.act, lambda: nc.scalar.copy(out=cv, in_=sv), [t_st], [t_cb])
        elif e is P.pool:
            P.op(P.pool, lambda: nc.gpsimd.tensor_copy(out=cv, in_=sv), [t_st], [t_cb])
        else:
            P.op(P.dve, lambda: nc.vector.tensor_copy(out=cv, in_=sv), [t_st], [t_cb])
        yield
        P.dma(P.pool, dst, cv, [t_cb], [tok], t_cb)
        yield


def convert_weights(nc, P, C, jobs):
    with ExitStack() as es:
        st = tiles(nc, es, "cv_st", [128, 2048], F32, 4)
        cb = tiles(nc, es, "cv_cb", [128, 2048], BF16, 4)
        run_interleaved([conv_stream(nc, P, C, jobs, st, cb, [P.dve, P.act])])
        P.barrier([j[2] for j in jobs[:1]])


def rms_stats(nc, P, C, src_ap, t_src, coef, small):
    junk, t_junk = C.junk.get()
    st, t_st = small.get()
    P.op(P.act, lambda: nc.scalar.activation(out=junk[:], in_=src_ap, func=AF.Square, scale=1.0 / 32, accum_out=st[:, 0:1]), [t_src], [t_junk, t_st])
    P.op(P.act, lambda: nc.scalar.activation(out=st[:, 1:2], in_=st[:, 0:1], func=AF.Ln, scale=1.0 / coef ** 2, bias=EPS / coef ** 2), [t_st], [t_st])
    P.op(P.act, lambda: nc.scalar.activation(out=st[:, 2:3], in_=st[:, 1:2], func=AF.Exp, scale=-0.5), [t_st], [t_st])
    return st[:, 2:3], t_st


def norm_transpose(nc, P, C, xt, t_xt, gbc, t_g, dstT, t_dst, small, xnp):
    rstd, t_r = rms_stats(nc, P, C, xt[:], t_xt, 1.0, small)
    xn, t_xn = xnp.get()
    P.op(P.dve, lambda: nc.vector.scalar_tensor_tensor(out=xn[:], in0=xt[:], scalar=rstd, in1=gbc[:], op0=ALU.mult, op1=ALU.mult), [t_xt, t_r, t_g], [t_xn])
    transpose8(nc, P, C, xn, t_xn, dstT, t_dst)


def transpose8(nc, P, C, xn, t_xn, dstT, t_dst, eng=None):
    ps, t_ps = C.psT.get()
    for kc in range(8):
        P.op(P.pe, lambda kc=kc: nc.tensor.transpose(out=ps[:, kc * 128:(kc + 1) * 128], in_=xn[:, kc * 128:(kc + 1) * 128], identity=C.idt[:]),
             [t_xn, C.t_idt], [t_ps], sig=(kc == 7))
    P.op(P.act, lambda: nc.scalar.copy(out=dstT, in_=ps[:].rearrange("p (k t) -> p k t", k=8)), [t_ps], [t_dst])


def post_residual(nc, P, C, y0, t_y0, y1, t_y1, gbc, t_g, xt, t_xt, coef, small, xop, dst_rows):
    ys, t_ys = C.ysp.get()
    P.op(P.act, lambda: nc.scalar.copy(out=ys[:, 0:512], in_=y0[:]), [t_y0], [t_ys])
    P.op(P.act, lambda: nc.scalar.copy(out=ys[:, 512:1024], in_=y1[:]), [t_y1], [t_ys])
    rstd, t_r = rms_stats(nc, P, C, ys[:], t_ys, coef, small)
    xo, t_xo = xop.get()
    P.op(P.dve, lambda: nc.vector.tensor_tensor(out=xo[:], in0=ys[:], in1=gbc[:], op=ALU.mult), [t_ys, t_g], [t_xo])
    P.op(P.dve, lambda: nc.vector.scalar_tensor_tensor(out=xo[:], in0=xo[:], scalar=rstd, in1=xt[:], op0=ALU.mult, op1=ALU.add), [t_xo, t_r, t_xt], [t_xo])
    P.dma(P.pool, dst_rows, xo[:], [t_xo], [C.t_xdst], t_xo)


def load_gain(nc, P, C, es, name, gvec):
    g = sb(nc, es, name, [128, 1024], F32)
    t = Tok()
    P.dma(P.sp, g[:], gvec.partition_broadcast(128), [], [t], t)
    return g, t


def ffn_phase(nc, P, C, src, dst, w1s, w2s, g_pre, g_post, TB=1024, bg_jobs=()):
    NT = TB // 128
    NH = TB // 512
    with ExitStack() as es:
        w2b = sb(nc, es, "w2b", [128, NFC, 1024], BF16)
        t_w2 = [Tok() for _ in range(NFC // 2)]
        xnTs = [sb(nc, es, f"xnT{i}", [128, 8, TB], BF16) for i in range(2)]
        t_xnTs = [[Tok() for _ in range(NT)] for _ in range(2)]
        aT = sb(nc, es, "aT", [128, NFC, TB], BF16)
        t_aT = [[Tok() for _ in range(NH)] for _ in range(NFC)]
        slabs = tiles(nc, es, "slab", [128, 8, 256], BF16, 3)
        xtp = tiles(nc, es, "xt", [128, 1024], F32, 3)
        xnp = tiles(nc, es, "xn", [128, 1024], BF16, 2)
        C.junk = tiles(nc, es, "junk", [128, 1024], BF16, 1)
        sgp = tiles(nc, es, "sg", [128, 512], F32, 3)
        xop = tiles(nc, es, "xo", [128, 1024], F32, 2)
        C.ysp = tiles(nc, es, "ys", [128, 1024], F32, 2)
        small = tiles(nc, es, "small", [128, 16], F32, 6)
        gpre, t_gpre = load_gain(nc, P, C, es, "gpre", g_pre)
        gpost, t_gpost = load_gain(nc, P, C, es, "gpost", g_post)
        for i in range(NFC // 2):
            P.dma(P.sp, w2b[:, 2 * i:2 * i + 2, :], w2s[:, 2 * i:2 * i + 2, :], [C.t_wscr], [t_w2[i]], t_w2[i])
        nblk = C.NTOK // TB
        bg = None
        if bg_jobs:
            cst = tiles(nc, es, "bg_st", [128, 2048], F32, 2)
            ccb = tiles(nc, es, "bg_cb", [128, 2048], BF16, 2)
            bg = conv_stream(nc, P, C, list(bg_jobs), cst, ccb, [P.pool])
        bg_steps = (3 * len(bg_jobs) + nblk - 1) // nblk if bg_jobs else 0

        def bg_chunk():
            for _ in range(bg_steps):
                try:
                    next(bg)
                except StopIteration:
                    return
                yield

        def stageA(blk):
            r0 = blk * TB
            xnT = xnTs[blk % 2]
            def load(tt):
                xt, t_xt = xtp.get()
                P.dma(P.sp, xt[:], src[r0 + tt * 128:r0 + (tt + 1) * 128, :], [C.t_xsrc], [t_xt], t_xt)
                return xt, t_xt
            nxt = load(0)
            yield
            for tt in range(NT):
                xt, t_xt = nxt
                if tt + 1 < NT:
                    nxt = load(tt + 1)
                rstd, t_r = rms_stats(nc, P, C, xt[:], t_xt, 1.0, small)
                xn, t_xn = xnp.get()
                P.op(P.dve, lambda: nc.vector.scalar_tensor_tensor(out=xn[:], in0=xt[:], scalar=rstd, in1=gpre[:], op0=ALU.mult, op1=ALU.mult), [t_xt, t_r, t_gpre], [t_xn])
                yield
                transpose8(nc, P, C, xn, t_xn, xnT[:, :, tt * 128:(tt + 1) * 128], t_xnTs[blk % 2][tt])
                yield

        def stageB(blk):
            xnT = xnTs[blk % 2]
            t_xnT = t_xnTs[blk % 2]
            pend = []

            def load_slab(fp):
                sl, t_sl = slabs.get()
                P.dma(P.sp, sl[:], w1s[fp], [C.t_wscr], [t_sl], t_sl)
                return sl, t_sl
            pend.append(load_slab(0))
            pend.append(load_slab(1))
            for fp in range(NFC):
                sl, t_sl = pend.pop(0)
                if fp + 2 < NFC:
                    pend.append(load_slab(fp + 2))
                for h in range(NH):
                    rd = [t_sl] + t_xnT[h * 4:(h + 1) * 4]
                    pg, t_pg = C.psA.get()
                    for kc in range(8):
                        P.op(P.pe, lambda kc=kc: nc.tensor.matmul(pg[:], lhsT=sl[:, kc, 0:128], rhs=xnT[:, kc, h * 512:(h + 1) * 512], start=(kc == 0), stop=(kc == 7)),
                             rd, [t_pg], sig=(kc == 7))
                    pu, t_pu = C.psA.get()
                    for kc in range(8):
                        P.op(P.pe, lambda kc=kc: nc.tensor.matmul(pu[:], lhsT=sl[:, kc, 128:256], rhs=xnT[:, kc, h * 512:(h + 1) * 512], start=(kc == 0), stop=(kc == 7)),
                             rd, [t_pu], sig=(kc == 7))
                    sg, t_sg = sgp.get()
                    P.op(P.act, lambda: nc.scalar.activation(out=sg[:], in_=pg[:], func=AF.Silu), [t_pg], [t_sg])
                    P.op(P.dve, lambda: nc.vector.tensor_tensor(out=aT[:, fp, h * 512:(h + 1) * 512], in0=pu[:], in1=sg[:], op=ALU.mult), [t_pu, t_sg], [t_aT[fp][h]])
                    yield

        def stageC(blk):
            r0 = blk * TB
            for tt in range(NT):
                xt, t_xt = xtp.get()
                rows = slice(r0 + tt * 128, r0 + (tt + 1) * 128)
                P.dma(P.sp, xt[:], src[rows, :], [C.t_xsrc], [t_xt], t_xt)
                ys = []
                for dh in range(2):
                    y, t_y = C.psA.get()
                    for fc in range(NFC):
                        P.op(P.pe, lambda fc=fc: nc.tensor.matmul(y[:], lhsT=aT[:, fc, tt * 128:(tt + 1) * 128], rhs=w2b[:, fc, dh * 512:(dh + 1) * 512], start=(fc == 0), stop=(fc == NFC - 1)),
                             [t_aT[fc][tt // 4], t_w2[fc // 2]], [t_y], sig=(fc == NFC - 1))
                    ys.append((y, t_y))
                post_residual(nc, P, C, ys[0][0], ys[0][1], ys[1][0], ys[1][1], gpost, t_gpost, xt, t_xt, 0.5, small, xop, dst[rows, :])

        run_interleaved([stageA(0)])
        for blk in range(nblk):
            run_interleaved([stageB(blk), stageA(blk + 1) if blk + 1 < nblk else None, bg_chunk() if bg is not None else None])
            stageC(blk)
        if bg is not None:
            for _ in bg:
                pass
        P.barrier([C.t_xdst, C.t_xsrc, C.t_wscr])


GLA_IN = 3104
DK = 128
DV = 256
NH = 4


def mat_jobs(w, scr, ncols):
    v = w.rearrange("(kc p) f -> p kc f", p=128)
    jobs = []
    for c0 in range(0, ncols, 256):
        c1 = min(ncols, c0 + 256)
        jobs.append(([((0, c1 - c0), v[:, :, c0:c1])], scr[:, :, c0:c1], "gen"))
    return jobs


def gla_sweep(nc, P, C, es, W, direction, src, dst, of_dram):
    fwd = direction == 0
    NTILE = C.NTOK // 128
    order = list(range(NTILE)) if fwd else list(range(NTILE - 1, -1, -1))
    chunks = [0, 1] if fwd else [1, 0]
    gdc = 3072 + 16 * direction
    for h in range(NH):
        P.op(P.dve, lambda h=h: nc.vector.memset(W.T[h][:], 0.0), [], [W.t_T[h]])
    d0, t_d0 = W.decp.get()
    P.op(P.dve, lambda: nc.vector.memset(d0[:], 1.0), [], [t_d0])
    state = dict(dprev=(d0, t_d0))

    def stage1a(tl, S):
        rows = slice(tl * 128, (tl + 1) * 128)
        xt, t_xt = W.xtp.get()
        P.dma(P.sp, xt[:], src[rows, :], [C.t_xsrc], [t_xt], t_xt)
        yield
        xnT, t_xnT = W.xnTp.get()
        norm_transpose(nc, P, C, xt, t_xt, W.g2, W.t_g2, xnT[:], t_xnT, W.small, W.xnp)
        yield
        rdx = [t_xnT, W.t_win]
        gp, t_gp = C.psA.get()
        for kc in range(8):
            P.op(P.pe, lambda kc=kc: nc.tensor.matmul(gp[0:16, 0:128], lhsT=W.win[:, kc, gdc:gdc + 16], rhs=xnT[:, kc, :], start=(kc == 0), stop=(kc == 7)), rdx, [t_gp], sig=(kc == 7))
        ga, t_ga = W.gap.get()
        P.op(P.act, lambda: nc.scalar.copy(out=ga[0:16, :], in_=gp[0:16, 0:128]), [t_gp], [t_ga])
        yield
        zp, t_zp = C.psA.get()
        P.op(P.pe, lambda: nc.tensor.matmul(zp[:], lhsT=ga[0:17, :], rhs=W.wg[direction][0:17, :], start=True, stop=True), [t_ga, W.t_wg], [t_zp])
        ee, t_ee = W.ep.get()
        P.op(P.act, lambda: nc.scalar.activation(out=ee[:], in_=zp[:], func=AF.Exp, scale=-1.0), [t_zp], [t_ee])
        ll, t_ll = W.lp.get()
        P.op(P.act, lambda: nc.scalar.activation(out=ll[:], in_=ee[:], func=AF.Ln, bias=1.0), [t_ee], [t_ll])
        yield
        bt, t_bt = C.psA.get()
        P.op(P.pe, lambda: nc.tensor.matmul(bt[:], lhsT=W.tri[direction][:], rhs=ll[:], start=True, stop=True), [W.t_tri, t_ll], [t_bt])
        bd, t_bd = C.psA.get()
        for h in range(NH):
            P.op(P.pe, lambda h=h: nc.tensor.matmul(bd[:, h * 16:(h + 1) * 16], lhsT=ll[:, h * 128:(h + 1) * 128], rhs=W.tri2[:], start=True, stop=True), [W.t_tri, t_ll], [t_bd], sig=(h == NH - 1))
        dec, t_dec = W.decp.get()
        P.op(P.act, lambda: nc.scalar.activation(out=dec[:].rearrange("p (h c) -> p h c", c=2), in_=bd[:, 0:64].rearrange("p (h e) -> p h e", e=16)[:, :, 0:2], func=AF.Exp), [t_bd], [t_dec])
        yield
        if C.dbg == "h1":
            return None
        eit, t_eit = W.eitp.get()
        P.op(P.act, lambda: nc.scalar.activation(out=eit[:], in_=bt[:], func=AF.Exp, scale=-1.0), [t_bt], [t_eit])
        ebt, t_ebt = W.ebTp.get()
        P.op(P.act, lambda: nc.scalar.activation(out=ebt[:], in_=bt[:], func=AF.Exp), [t_bt], [t_ebt])
        yield
        S.update(tl=tl, rows=rows, xt=xt, t_xt=t_xt, xnT=xnT, t_xnT=t_xnT, rdx=rdx, eit=eit, t_eit=t_eit, ebt=ebt, t_ebt=t_ebt, dec=dec, t_dec=t_dec)
        yield

    def stage1b(S):
        xnT = S['xnT']; t_xnT = S['t_xnT']; rdx = S['rdx']; eit = S['eit']; t_eit = S['t_eit']; ebt = S['ebt']; t_ebt = S['t_ebt']
        kt, t_kt = C.psA.get()
        for kc in range(8):
            P.op(P.pe, lambda kc=kc: nc.tensor.matmul(kt[:], lhsT=xnT[:, kc, :], rhs=W.win[:, kc, 512:1024], start=(kc == 0), stop=(kc == 7)), rdx, [t_kt], sig=(kc == 7))
        kit, t_kit = W.kitp.get()
        P.op(P.dve, lambda: nc.vector.tensor_tensor(out=kit[:], in0=kt[:], in1=eit[:], op=ALU.mult), [t_kt, t_eit], [t_kit])
        yield
        qt, t_qt = C.psA.get()
        for kc in range(8):
            P.op(P.pe, lambda kc=kc: nc.tensor.matmul(qt[:], lhsT=xnT[:, kc, :], rhs=W.win[:, kc, 0:512], start=(kc == 0), stop=(kc == 7)), rdx, [t_qt], sig=(kc == 7))
        qdt, t_qdt = W.qdtp.get()
        P.op(P.dve, lambda: nc.vector.scalar_tensor_tensor(out=qdt[:], in0=qt[:], scalar=float(DK ** -0.5), in1=ebt[:], op0=ALU.mult, op1=ALU.mult), [t_qt, t_ebt], [t_qdt])
        yield
        vb, t_vb = W.vbp.get()
        for j in range(2):
            vp, t_vp = C.psA.get()
            for kc in range(8):
                P.op(P.pe, lambda kc=kc: nc.tensor.matmul(vp[:], lhsT=xnT[:, kc, :], rhs=W.win[:, kc, 1024 + j * 512:1536 + j * 512], start=(kc == 0), stop=(kc == 7)), rdx, [t_vp], sig=(kc == 7))
            P.op(P.act, lambda: nc.scalar.copy(out=vb[:, j * 512:(j + 1) * 512], in_=vp[:]), [t_vp], [t_vb])
            yield
        if C.dbg == "h2":
            return None
        pst, t_pst = C.psT.get()
        for h in range(NH):
            P.op(P.pe, lambda h=h: nc.tensor.transpose(out=pst[:, h * 128:(h + 1) * 128], in_=kit[:, h * 128:(h + 1) * 128], identity=C.idt[:]), [t_kit, C.t_idt], [t_pst], sig=False)
        for h in range(NH):
            P.op(P.pe, lambda h=h: nc.tensor.transpose(out=pst[:, 512 + h * 128:512 + (h + 1) * 128], in_=qdt[:, h * 128:(h + 1) * 128], identity=C.idt[:]), [t_qdt, C.t_idt], [t_pst], sig=(h == NH - 1))
        kiT, t_kiT = W.kiTp.get()
        P.op(P.act, lambda: nc.scalar.copy(out=kiT[:], in_=pst[:, 0:512]), [t_pst], [t_kiT])
        if C.dbg == "h3":
            return None
        v3 = lambda ap: ap.rearrange("p (h t) -> p h t", h=NH)
        qd, t_qd = W.qdp.get()
        for c in range(2):
            P.op(P.act, lambda c=c: nc.scalar.copy(out=qd[:, c, :, c * 64:(c + 1) * 64], in_=v3(pst[:, 512:1024])[:, :, c * 64:(c + 1) * 64]), [t_pst], [t_qd])
        yield
        ap_, t_ap = C.psA.get()
        for h in range(NH):
            for c in range(2):
                P.op(P.pe, lambda h=h, c=c: nc.tensor.matmul(ap_[:, h * 128 + c * 64:h * 128 + (c + 1) * 64], lhsT=kiT[:, h * 128:(h + 1) * 128], rhs=qd[:, c, h, c * 64:(c + 1) * 64], start=True, stop=True),
                     [t_kiT, t_qd], [t_ap], sig=(h == NH - 1 and c == 1))
        am, t_am = W.amp.get()
        P.op(P.dve, lambda: nc.vector.tensor_tensor(out=am[:], in0=ap_[:], in1=W.mask4[direction][:], op=ALU.mult), [t_ap, W.t_mask], [t_am])
        S.update(kit=kit, t_kit=t_kit, vb=vb, t_vb=t_vb, qd=qd, t_qd=t_qd, am=am, t_am=t_am)
        yield

    def stage2(S, it):
        tl = S['tl']; rows = S['rows']; xt = S['xt']; t_xt = S['t_xt']; xnT = S['xnT']; t_xnT = S['t_xnT']; rdx = S['rdx']
        kit = S['kit']; t_kit = S['t_kit']; vb = S['vb']; t_vb = S['t_vb']; qd = S['qd']; t_qd = S['t_qd']; am = S['am']; t_am = S['t_am']; dec = S['dec']; t_dec = S['t_dec']
        dprev, t_dprev = state['dprev']
        if it > 0 and (tl % C.seg_tiles == (0 if fwd else C.seg_tiles - 1)):
            P.op(P.dve, lambda: nc.vector.tensor_scalar(out=dprev[:], in0=dprev[:], scalar1=W.cf[:, 0:1], scalar2=None, op0=ALU.mult), [t_dprev, W.t_cf], [t_dprev])
        if not fwd:
            of, t_of = W.ofp.get()
            P.dma(P.sp, of[:], of_dram[rows, :], [C.t_of], [t_of], t_of)
        ops = []
        for hp in range(2):
            o_, t_o = C.psO.get()
            ops.append((o_, t_o))
        sbs = {}
        for ci, c in enumerate(chunks):
            for h in range(NH):
                dsc = dprev[:, h * 2 + chunks[1]:h * 2 + chunks[1] + 1] if ci == 0 else dec[:, h * 2 + chunks[0]:h * 2 + chunks[0] + 1]
                t_ds = t_dprev if ci == 0 else t_dec
                sb_, t_sb = W.sbp.get()
                P.op(P.dve, lambda sb_=sb_, dsc=dsc: nc.vector.tensor_scalar(out=sb_[:], in0=W.T[h][:], scalar1=dsc, scalar2=None, op0=ALU.mult), [W.t_T[h], t_ds], [t_sb])
                sbs[(h, c)] = (sb_, t_sb)
            kvs = []
            for h in range(NH):
                kv, t_kv = C.psA.get()
                P.op(P.pe, lambda c=c, kv=kv: nc.tensor.matmul(kv[:, 0:256], lhsT=kit[c * 64:(c + 1) * 64, h * 128:(h + 1) * 128], rhs=vb[c * 64:(c + 1) * 64, h * 256:(h + 1) * 256], start=True, stop=True),
                     [t_kit, t_vb], [t_kv])
                kvs.append((kv, t_kv))
            for h in range(NH):
                dsc = dprev[:, h * 2 + chunks[1]:h * 2 + chunks[1] + 1] if ci == 0 else dec[:, h * 2 + chunks[0]:h * 2 + chunks[0] + 1]
                t_ds = t_dprev if ci == 0 else t_dec
                kv, t_kv = kvs[h]
                P.op(P.dve, lambda dsc=dsc, kv=kv: nc.vector.scalar_tensor_tensor(out=W.T[h][:], in0=W.T[h][:], scalar=dsc, in1=kv[:, 0:256], op0=ALU.mult, op1=ALU.add), [W.t_T[h], t_ds, t_kv], [W.t_T[h]])
            yield
        for h in range(NH):
            o_, t_o = ops[h // 2]
            osl = o_[:, (h % 2) * 256:(h % 2 + 1) * 256]
            P.op(P.pe, lambda: nc.tensor.matmul(osl, lhsT=am[:, h * 128:(h + 1) * 128], rhs=vb[:, h * 256:(h + 1) * 256], start=True, stop=False), [t_am, t_vb], [t_o], sig=False)
            for k_, c in enumerate(chunks):
                sb_, t_sb = sbs[(h, c)]
                P.op(P.pe, lambda c=c, sb_=sb_: nc.tensor.matmul(osl, lhsT=qd[:, c, h, :], rhs=sb_[:], start=False, stop=(k_ == 1)), [t_qd, t_sb], [t_o], sig=(k_ == 1))
            if h % 2 == 1:
                yield
        state['dprev'] = (dec, t_dec)
        yield
        if fwd:
            of, t_of = W.ofp.get()
            for hp in range(2):
                P.op(P.act, lambda hp=hp: nc.scalar.copy(out=of[:, hp * 512:(hp + 1) * 512], in_=ops[hp][0][:]), [ops[hp][1]], [t_of])
            P.dma(P.pool, of_dram[rows, :], of[:], [t_of], [C.t_of], t_of)
            return
        for hp in range(2):
            P.op(P.dve, lambda hp=hp: nc.vector.tensor_tensor(out=of[:, hp * 512:(hp + 1) * 512], in0=ops[hp][0][:], in1=of[:, hp * 512:(hp + 1) * 512], op=ALU.add), [ops[hp][1], t_of], [t_of])
        S['of'] = of
        S['t_of'] = t_of
        yield

    def stage2b(S):
        rows = S['rows']; xt = S['xt']; t_xt = S['t_xt']; xnT = S['xnT']; t_xnT = S['t_xnT']; rdx = S['rdx']; of = S['of']; t_of = S['t_of']
        junk, t_junk = C.junk.get()
        st, t_st = W.small.get()
        for h in range(NH):
            P.op(P.act, lambda h=h: nc.scalar.activation(out=junk[:, h * 256:(h + 1) * 256], in_=of[:, h * 256:(h + 1) * 256], func=AF.Square, scale=1.0 / 16, accum_out=st[:, h * 8:h * 8 + 1]), [t_of], [t_junk, t_st])
        s4 = lambda o: st[:].rearrange("p (h e) -> p h e", e=8)[:, 0:4, o]
        P.op(P.act, lambda: nc.scalar.activation(out=s4(1), in_=s4(0), func=AF.Ln, bias=EPS), [t_st], [t_st])
        P.op(P.act, lambda: nc.scalar.activation(out=s4(2), in_=s4(1), func=AF.Exp, scale=-0.5), [t_st], [t_st])
        on, t_on = W.onp.get()
        for h in range(NH):
            P.op(P.dve, lambda h=h: nc.vector.scalar_tensor_tensor(out=on[:, h * 256:(h + 1) * 256], in0=of[:, h * 256:(h + 1) * 256], scalar=st[:, h * 8 + 2:h * 8 + 3], in1=W.gon[:, h * 256:(h + 1) * 256], op0=ALU.mult, op1=ALU.mult),
                 [t_of, t_st, W.t_gon], [t_on])
        yield
        sr, t_sr = W.srp.get()
        for j in range(2):
            rp, t_rp = C.psA.get()
            for kc in range(8):
                P.op(P.pe, lambda kc=kc: nc.tensor.matmul(rp[:], lhsT=xnT[:, kc, :], rhs=W.win[:, kc, 2048 + j * 512:2560 + j * 512], start=(kc == 0), stop=(kc == 7)), rdx, [t_rp], sig=(kc == 7))
            P.op(P.act, lambda: nc.scalar.activation(out=sr[:, j * 512:(j + 1) * 512], in_=rp[:], func=AF.Silu), [t_rp], [t_sr])
            yield
        og, t_og = W.xnp.get()
        P.op(P.dve, lambda: nc.vector.tensor_tensor(out=og[:], in0=on[:], in1=sr[:], op=ALU.mult), [t_on, t_sr], [t_og])
        ogT, t_ogT = W.xnTp.get()
        transpose8(nc, P, C, og, t_og, ogT[:], t_ogT)
        yield
        ys = []
        for dh in range(2):
            y, t_y = C.psA.get()
            for kc in range(8):
                P.op(P.pe, lambda kc=kc: nc.tensor.matmul(y[:], lhsT=ogT[:, kc, :], rhs=W.wout[:, kc, dh * 512:(dh + 1) * 512], start=(kc == 0), stop=(kc == 7)), [t_ogT, W.t_wout], [t_y], sig=(kc == 7))
            ys.append((y, t_y))
        post_residual(nc, P, C, ys[0][0], ys[0][1], ys[1][0], ys[1][1], W.g3, W.t_g3, xt, t_xt, 1.0, W.small, W.xop, dst[rows, :])
        yield

    Ss = [dict() for _ in range(NTILE)]
    run_interleaved([stage1a(order[0], Ss[0])])
    run_interleaved([stage1b(Ss[0]), stage1a(order[1], Ss[1]) if NTILE > 1 else None])
    for it in range(NTILE):
        run_interleaved([stage2(Ss[it], it),
                         stage2b(Ss[it - 1]) if (not fwd and it > 0) else None,
                         stage1b(Ss[it + 1]) if it + 1 < NTILE else None,
                         stage1a(order[it + 2], Ss[it + 2]) if it + 2 < NTILE else None])
        if it > 0:
            Ss[it - 1] = None
    if not fwd:
        run_interleaved([stage2b(Ss[NTILE - 1])])


def gla_layer(nc, P, C, src, dst, j, g2vec, g3vec):
    with ExitStack() as es:
        W = Ctx()
        W.win = sb(nc, es, "gwin", [128, 8, GLA_IN], BF16)
        W.t_win = Tok()
        for c0 in range(0, GLA_IN, 1024):
            c1 = min(GLA_IN, c0 + 1024)
            P.dma(P.sp, W.win[:, :, c0:c1], C.gwin_s[j][:, :, c0:c1], [C.t_wscr], [W.t_win], W.t_win)
        W.wout = sb(nc, es, "gwout", [128, 8, 1024], BF16)
        W.t_wout = Tok()
        P.dma(P.sp, W.wout[:], C.gwout_s[j], [C.t_wscr], [W.t_wout], W.t_wout)
        W.wg = [sb(nc, es, f"wg{d}", [32, 512], BF16) for d in range(2)]
        W.t_wg = Tok()
        with ExitStack() as es2:
            wgs = sb(nc, es2, "wgs", [32, 2, 512], F32)
            t_wgs = Tok()
            for d in range(2):
                P.dma(P.sp, wgs[0:16, d, :], C.gate_w[d][j], [], [t_wgs], t_wgs)
                P.dma(P.sp, wgs[16:17, d, :], C.gate_b[d][j:j + 1, :], [], [t_wgs], t_wgs)
            for d in range(2):
                P.op(P.dve, lambda d=d: nc.vector.tensor_copy(out=W.wg[d][0:17, :], in_=wgs[0:17, d, :]), [t_wgs], [W.t_wg])
            P.barrier([t_wgs])
        W.tri = [sb(nc, es, f"tri{d}", [128, 128], F32) for d in range(2)]
        W.t_tri = Tok()
        W.mask4 = [sb(nc, es, f"mask{d}", [128, 512], F32) for d in range(2)]
        W.t_mask = Tok()
        for d in range(2):
            P.dma(P.sp, W.tri[d][:], C.tri_in[d], [], [W.t_tri], W.t_tri)
            P.dma(P.sp, W.mask4[d][:], C.mask_in[d], [], [W.t_mask], W.t_mask)
        W.tri2 = sb(nc, es, "tri2", [128, 16], F32)
        P.dma(P.sp, W.tri2[:], C.tri2_in, [], [W.t_tri], W.t_tri)
        W.qdtp = tiles(nc, es, "qdt", [128, 512], BF16, 2)
        W.cf = sb(nc, es, "cf", [128, 1], F32)
        W.t_cf = Tok()
        P.dma(P.sp, W.cf[:], C.cf_in, [], [W.t_cf], W.t_cf)
        W.g2, W.t_g2 = load_gain(nc, P, C, es, "g2", g2vec)
        W.g3, W.t_g3 = load_gain(nc, P, C, es, "g3", g3vec)
        W.gon, W.t_gon = load_gain(nc, P, C, es, "gon", C.gla_onorm[j])
        W.T = [sb(nc, es, f"T{h}", [128, 256], F32) for h in range(NH)]
        W.t_T = [Tok() for _ in range(NH)]
        W.xtp = tiles(nc, es, "xt", [128, 1024], F32, 5)
        W.xnp = tiles(nc, es, "xn", [128, 1024], BF16, 2)
        W.xnTp = tiles(nc, es, "xnT", [128, 8, 128], BF16, 8)
        C.junk = tiles(nc, es, "junk", [128, 1024], BF16, 1)
        W.small = tiles(nc, es, "small", [128, 32], F32, 4)
        W.vbp = tiles(nc, es, "vb", [128, 1024], BF16, 3)
        W.gap = tiles(nc, es, "ga", [32, 128], BF16, 2)
        for g_ in W.gap.tiles:
            P.op(P.dve, lambda g_=g_: nc.vector.memset(g_[:], 1.0), [], [W.gap.toks[W.gap.tiles.index(g_)]])
        W.ep = tiles(nc, es, "ee", [128, 512], F32, 2)
        W.lp = tiles(nc, es, "ll", [128, 512], F32, 2)
        W.eitp = tiles(nc, es, "eit", [128, 512], F32, 3)
        W.ebTp = tiles(nc, es, "ebT", [128, 512], F32, 3)
        W.kitp = tiles(nc, es, "kit", [128, 512], BF16, 3)
        W.qdp = tiles(nc, es, "qd", [128, 2, NH, 128], BF16, 3)
        for q_ in W.qdp.tiles:
            P.op(P.dve, lambda q_=q_: nc.vector.memset(q_[:], 0.0), [], [W.qdp.toks[W.qdp.tiles.index(q_)]])
        W.kiTp = tiles(nc, es, "kiT", [128, 512], BF16, 2)
        W.decp = tiles(nc, es, "dec", [128, 8], F32, 6)
        W.amp = tiles(nc, es, "am", [128, 512], BF16, 3)
        W.sbp = tiles(nc, es, "sbs", [128, 256], BF16, 8)
        W.ofp = tiles(nc, es, "of", [128, 1024], F32, 2)
        W.onp = tiles(nc, es, "on", [128, 1024], F32, 1)
        W.srp = tiles(nc, es, "sr", [128, 1024], F32, 1)
        W.xop = tiles(nc, es, "xo", [128, 1024], F32, 2)
        C.ysp = tiles(nc, es, "ys", [128, 1024], F32, 1)
        gla_sweep(nc, P, C, es, W, 0, src, dst, C.of_dram)
        P.barrier([C.t_of, C.t_xsrc])
        gla_sweep(nc, P, C, es, W, 1, src, dst, C.of_dram)
        P.barrier([C.t_xdst, C.t_xsrc, C.t_wscr, C.t_of])


SWA_IN = 1536
NEG = -1e30


def transposeN(nc, P, C, src, t_src, n, dstT, t_dst):
    ps, t_ps = C.psT.get()
    for k in range(n):
        P.op(P.pe, lambda k=k: nc.tensor.transpose(out=ps[:, k * 128:(k + 1) * 128], in_=src[:, k * 128:(k + 1) * 128], identity=C.idt[:]),
             [t_src, C.t_idt], [t_ps], sig=(k == n - 1))
    P.op(P.act, lambda: nc.scalar.copy(out=dstT, in_=ps[:, 0:n * 128].rearrange("p (k t) -> p k t", k=n)), [t_ps], [t_dst])


def rope_inplace(nc, P, W, xs, t_xs, nheads, rt, t_rt):
    xv = xs[:, 0:nheads * 64].rearrange("p (h d) -> p h d", d=64)
    x1 = xv[:, :, 0:8]
    x2 = xv[:, :, 8:16]
    cos = rt[:, 0:nheads * 8].rearrange("p (h d) -> p h d", d=8)
    sin = rt[:, 128:128 + nheads * 8].rearrange("p (h d) -> p h d", d=8)
    tp, t_tp = W.ropet.get()
    tv = lambda k: tp[:, k, 0:nheads * 8].rearrange("p (h d) -> p h d", d=8)
    P.op(P.dve, lambda: nc.vector.tensor_tensor(out=tv(0), in0=x1, in1=cos, op=ALU.mult), [t_xs, t_rt], [t_tp])
    P.op(P.dve, lambda: nc.vector.tensor_tensor(out=tv(1), in0=x2, in1=sin, op=ALU.mult), [t_xs, t_rt], [t_tp])
    P.op(P.dve, lambda: nc.vector.tensor_tensor(out=tv(2), in0=x2, in1=cos, op=ALU.mult), [t_xs, t_rt], [t_tp])
    P.op(P.dve, lambda: nc.vector.tensor_tensor(out=tv(3), in0=x1, in1=sin, op=ALU.mult), [t_xs, t_rt], [t_tp])
    P.op(P.dve, lambda: nc.vector.tensor_tensor(out=x1, in0=tv(0), in1=tv(1), op=ALU.subtract), [t_tp], [t_xs])
    P.op(P.dve, lambda: nc.vector.tensor_tensor(out=x2, in0=tv(2), in1=tv(3), op=ALU.add), [t_tp], [t_xs])


def swa_layer(nc, P, C, src, dst, j, g2vec, g3vec):
    NTILE = C.NTOK // 128
    ST = C.seg_tiles
    with ExitStack() as es:
        W = Ctx()
        W.win = sb(nc, es, "swin", [128, 8, SWA_IN], BF16)
        W.t_win = Tok()
        P.dma(P.sp, W.win[:], C.swin_s[j], [C.t_wscr], [W.t_win], W.t_win)
        W.wout = sb(nc, es, "swout", [128, 8, 1024], BF16)
        W.t_wout = Tok()
        P.dma(P.sp, W.wout[:], C.swout_s[j], [C.t_wscr], [W.t_wout], W.t_wout)
        masks = sb(nc, es, "smask", [128, 3, 4, 384], F32)
        t_masks = Tok()
        P.dma(P.sp, masks[:], C.swa_mask_in, [], [t_masks], t_masks)
        sink = sb(nc, es, "sink", [128, 16], F32)
        t_sink = Tok()
        P.dma(P.sp, sink[:], C.swa_sinks[j].partition_broadcast(128), [], [t_sink], t_sink)
        W.g2, W.t_g2 = load_gain(nc, P, C, es, "g2", g2vec)
        W.g3, W.t_g3 = load_gain(nc, P, C, es, "g3", g3vec)
        xtp = tiles(nc, es, "xt", [128, 1024], F32, 5)
        xnp = tiles(nc, es, "xn", [128, 1024], BF16, 5)
        xnTp = tiles(nc, es, "xnT", [128, 8, 128], BF16, 3)
        C.junk = tiles(nc, es, "junk", [128, 1024], BF16, 1)
        small = tiles(nc, es, "small", [128, 32], F32, 4)
        qsp = tiles(nc, es, "qs", [128, 1024], F32, 2)
        ksp = tiles(nc, es, "ks", [128, 256], F32, 2)
        rtp = tiles(nc, es, "rt", [128, 256], F32, 2)
        W.ropet = tiles(nc, es, "ropet", [128, 4, 128], F32, 2)
        kb2p = tiles(nc, es, "kb2", [128, 4, 2, 64], BF16, 2)
        qTp = tiles(nc, es, "qT", [128, 8, 128], BF16, 5)
        kT2p = tiles(nc, es, "kT2", [128, 4, 128], BF16, 6)
        vbp = tiles(nc, es, "vb", [128, 256], BF16, 6)
        smp = tiles(nc, es, "sm", [128, 4, 392], F32, 4)
        pbp = tiles(nc, es, "pb", [128, 4, 392], BF16, 4)
        pTp = tiles(nc, es, "pT", [128, 4, 384], BF16, 4)
        stp = tiles(nc, es, "sst", [128, 16], F32, 5)
        xop = tiles(nc, es, "xo", [128, 1024], F32, 2)
        C.ysp = tiles(nc, es, "ys", [128, 1024], F32, 2)
        st8 = {}
        epi = {}

        def inproj(tl):
            rows = slice(tl * 128, (tl + 1) * 128)
            xt, t_xt = xtp.get()
            P.dma(P.sp, xt[:], src[rows, :], [C.t_xsrc], [t_xt], t_xt)
            rt, t_rt = rtp.get()
            P.dma(P.sp, rt[:], C.rope_in[rows, :], [], [t_rt], t_rt)
            yield
            xnT, t_xnT = xnTp.get()
            norm_transpose(nc, P, C, xt, t_xt, W.g2, W.t_g2, xnT[:], t_xnT, small, xnp)
            yield
            rdx = [t_xnT, W.t_win]
            qs, t_qs = qsp.get()
            for jh in range(2):
                qp, t_qp = C.psA.get()
                for kc in range(8):
                    P.op(P.pe, lambda kc=kc: nc.tensor.matmul(qp[:], lhsT=xnT[:, kc, :], rhs=W.win[:, kc, jh * 512:(jh + 1) * 512], start=(kc == 0), stop=(kc == 7)), rdx, [t_qp], sig=(kc == 7))
                P.op(P.act, lambda: nc.scalar.activation(out=qs[:, jh * 512:(jh + 1) * 512], in_=qp[:], func=AF.Copy, scale=0.125), [t_qp], [t_qs])
                yield
            kvp, t_kvp = C.psA.get()
            for kc in range(8):
                P.op(P.pe, lambda kc=kc: nc.tensor.matmul(kvp[:], lhsT=xnT[:, kc, :], rhs=W.win[:, kc, 1024:1536], start=(kc == 0), stop=(kc == 7)), rdx, [t_kvp], sig=(kc == 7))
            ks, t_ks = ksp.get()
            P.op(P.act, lambda: nc.scalar.copy(out=ks[:], in_=kvp[:, 0:256]), [t_kvp], [t_ks])
            vb, t_vb = vbp.get()
            P.op(P.act, lambda: nc.scalar.copy(out=vb[:], in_=kvp[:, 256:512]), [t_kvp], [t_vb])
            yield
            rope_inplace(nc, P, W, qs, t_qs, 16, rt, t_rt)
            yield
            rope_inplace(nc, P, W, ks, t_ks, 4, rt, t_rt)
            yield
            qb, t_qb = xnp.get()
            P.op(P.act, lambda: nc.scalar.copy(out=qb[:], in_=qs[:]), [t_qs], [t_qb])
            qT, t_qT = qTp.get()
            transpose8(nc, P, C, qb, t_qb, qT[:], t_qT)
            yield
            kb2, t_kb2 = kb2p.get()
            for d_ in range(2):
                P.op(P.dve, lambda d_=d_: nc.vector.tensor_copy(out=kb2[:, :, d_, :], in_=ks[:].rearrange("p (g d) -> p g d", d=64)), [t_ks], [t_kb2])
            kT2, t_kT2 = kT2p.get()
            transposeN(nc, P, C, kb2[:].rearrange("p g d e -> p (g d e)"), t_kb2, 4, kT2[:], t_kT2)
            st8[tl] = dict(xt=(xt, t_xt), qT=(qT, t_qT), kT2=(kT2, t_kT2), vb=(vb, t_vb))
            yield

        def group(tl, g, keys, msk, ob, on, t_on):
            nk = len(keys)
            Wd = nk * 128
            qT, t_qT = st8[tl]["qT"]
            C.psA.i = 0
            banks = [C.psA.get() for _ in range(4)]
            t_banks = [b[1] for b in banks]
            for hh in range(4):
                hq = g * 4 + hh
                p_, half = hq // 2, hq % 2
                pr = slice(half * 64, (half + 1) * 64)
                for jj, kt_ in enumerate(keys):
                    kT2, t_kT2 = st8[kt_]["kT2"]
                    P.op(P.pe, lambda jj=jj, kT2=kT2: nc.tensor.matmul(C.psS[:, hh, jj * 128:(jj + 1) * 128], lhsT=qT[pr, p_, :], rhs=kT2[pr, g, :], start=True, stop=True),
                         [t_qT, t_kT2], [t_banks[hh]], sig=(jj == nk - 1))
            sm, t_sm = smp.get()
            P.op(P.dve, lambda: nc.vector.tensor_tensor(out=sm[:, :, 0:Wd], in0=C.psS[:, :, 0:Wd], in1=msk, op=ALU.add), t_banks + [t_masks], [t_sm])
            P.op(P.act, lambda: nc.scalar.copy(out=sm[:, :, Wd], in_=sink[:, g * 4:(g + 1) * 4]), [t_sink], [t_sm])
            yield
            st, t_st = stp.get()
            P.op(P.dve, lambda: nc.vector.reduce_max(out=st[:, 0:4], in_=sm[:, :, 0:Wd + 1], axis=AX.X, negate=True), [t_sm], [t_st])
            yield
            pb, t_pb = pbp.get()
            for hh in range(4):
                P.op(P.act, lambda hh=hh: nc.scalar.activation(out=pb[:, hh, 0:Wd + 1], in_=sm[:, hh, 0:Wd + 1], func=AF.Exp, bias=st[:, hh:hh + 1], accum_out=st[:, 8 + hh:9 + hh]), [t_sm, t_st], [t_pb, t_st])
                if hh % 2 == 1:
                    yield
            P.op(P.dve, lambda: nc.vector.reciprocal(out=st[:, 4:8], in_=st[:, 8:12]), [t_st], [t_st])
            pT, t_pT = pTp.get()
            for h2 in range(2):
                ps, t_ps = C.psT.get()
                for hh in range(2):
                    for jj in range(nk):
                        P.op(P.pe, lambda hh=hh, jj=jj: nc.tensor.transpose(out=ps[:, (hh * nk + jj) * 128:(hh * nk + jj + 1) * 128], in_=pb[:, h2 * 2 + hh, jj * 128:(jj + 1) * 128], identity=C.idt[:]),
                             [t_pb, C.t_idt], [t_ps], sig=(hh == 1 and jj == nk - 1))
                P.op(P.act if h2 == 0 else P.dve, (lambda: nc.scalar.copy(out=pT[:, h2 * 2:h2 * 2 + 2, 0:Wd], in_=ps[:, 0:2 * Wd].rearrange("p (h w) -> p h w", h=2))) if h2 == 0 else
                     (lambda: nc.vector.tensor_copy(out=pT[:, h2 * 2:h2 * 2 + 2, 0:Wd], in_=ps[:, 0:2 * Wd].rearrange("p (h w) -> p h w", h=2))), [t_ps], [t_pT])
                yield
            o_, t_o = ob[g // 2]
            for hh in range(4):
                hq = g * 4 + hh
                osl = o_[:, (hq % 8) * 64:(hq % 8 + 1) * 64]
                for jj, kt_ in enumerate(keys):
                    vb, t_vb = st8[kt_]["vb"]
                    P.op(P.pe, lambda jj=jj, vb=vb: nc.tensor.matmul(osl, lhsT=pT[:, hh, jj * 128:(jj + 1) * 128], rhs=vb[:, g * 64:(g + 1) * 64], start=(jj == 0), stop=(jj == nk - 1)),
                         [t_pT, t_vb], [t_o], sig=(jj == nk - 1 and hh == 3))
            gsl = slice((g % 2) * 256, (g % 2 + 1) * 256)
            P.op(P.dve, lambda: nc.vector.tensor_tensor(out=on[:, g * 256:(g + 1) * 256].rearrange("p (h d) -> p h d", h=4), in0=o_[:, gsl].rearrange("p (h d) -> p h d", h=4),
                                                        in1=st[:, 4:8].unsqueeze(2).broadcast_to([128, 4, 64]), op=ALU.mult), [t_o, t_st], [t_on])
            yield

        def attn(tl):
            rows = slice(tl * 128, (tl + 1) * 128)
            S = st8[tl]
            keys = [t for t in (tl - 1, tl, tl + 1) if 0 <= t < NTILE]
            nk = len(keys)
            Wd = nk * 128
            mk = 0
            if tl % ST == 0 and tl > 0:
                mk = 1
            if tl % ST == ST - 1 and tl < NTILE - 1:
                mk = 2
            c0 = 0 if tl > 0 else 128
            msk = masks[:, mk, :, c0:c0 + Wd]
            ob = [C.psO.get() for _ in range(2)]
            on, t_on = xnp.get()
            ggens = [group(tl, g, keys, msk, ob, on, t_on) for g in range(4)]
            while ggens:
                for gg in list(ggens):
                    try:
                        next(gg)
                    except StopIteration:
                        ggens.remove(gg)
                    yield
            epi[tl] = (on, t_on)

        def epilogue(tl):
            rows = slice(tl * 128, (tl + 1) * 128)
            on, t_on = epi.pop(tl)
            onT, t_onT = xnTp.get()
            transpose8(nc, P, C, on, t_on, onT[:], t_onT)
            yield
            ys = []
            for dh in range(2):
                y, t_y = C.psA.get()
                for kc in range(8):
                    P.op(P.pe, lambda kc=kc: nc.tensor.matmul(y[:], lhsT=onT[:, kc, :], rhs=W.wout[:, kc, dh * 512:(dh + 1) * 512], start=(kc == 0), stop=(kc == 7)), [t_onT, W.t_wout], [t_y], sig=(kc == 7))
                ys.append((y, t_y))
            xt, t_xt = st8[tl]["xt"]
            post_residual(nc, P, C, ys[0][0], ys[0][1], ys[1][0], ys[1][1], W.g3, W.t_g3, xt, t_xt, 1.0, small, xop, dst[rows, :])
            st8.pop(tl - 1, None)
            yield

        run_interleaved([inproj(0)])
        for k_ in (1, 2):
            if NTILE > k_:
                run_interleaved([inproj(k_)])
        for tl in range(NTILE):
            run_interleaved([attn(tl), epilogue(tl - 1) if tl > 0 else None, inproj(tl + 3) if tl + 3 < NTILE else None])
        run_interleaved([epilogue(NTILE - 1)])
        P.barrier([C.t_xdst, C.t_xsrc, C.t_wscr])


def w1_jobs(w1, w1s):
    v = w1.rearrange("(kc p) f -> p kc f", p=128)
    jobs = []
    for fp in range(NFC):
        srcs = [((0, 128), v[:, :, fp * 128:(fp + 1) * 128]), ((128, 256), v[:, :, DFF + fp * 128:DFF + (fp + 1) * 128])]
        jobs.append((srcs, w1s[fp], "gen"))
    return jobs


def build(cfg):
    nc = bass.Bass("TRN2", target_bir_lowering=False)
    P = Prog(nc)
    C = Ctx()
    NTOK = cfg["NTOK"]
    C.NTOK = NTOK
    C.dbg = cfg.get("dbg")
    din = lambda name, shape, dt=F32: nc.dram_tensor(name, shape, dt, kind="ExternalInput").ap()
    dscr = lambda name, shape, dt=F32: nc.dram_tensor(name, shape, dt, kind="Internal").ap()
    x_in = din("x", [NTOK, D])
    norm_g = din("norm_g", [4, 6, D])
    NL = cfg.get("NL", 4)
    ffn_w1 = din("ffn_w1", [NL, 2, D, 2 * DFF])
    ffn_w2 = din("ffn_w2", [NL, 2, DFF, D])
    C.ident = din("ident", [128, 128], BF16)
    NG = cfg.get("NG", 2)
    gla_w_in = din("gla_w_in", [NG, D, GLA_IN])
    C.gate_w = [din("gla_w_gate_f", [NG, 16, 512]), din("gla_w_gate_b", [NG, 16, 512])]
    C.gate_b = [din("gla_b_gate_f", [NG, 512]), din("gla_b_gate_b", [NG, 512])]
    C.gla_onorm = din("gla_onorm", [NG, D])
    gla_w_out = din("gla_w_out", [NG, D, D])
    C.tri_in = [din("tri_f", [128, 128]), din("tri_b", [128, 128])]
    C.mask_in = [din("mask_f", [128, 512]), din("mask_b", [128, 512])]
    C.cf_in = din("cf", [128, 1])
    C.tri2_in = din("tri2", [128, 16])
    C.seg_tiles = cfg.get("seg_tiles", 16)
    C.of_dram = dscr("of_dram", [NTOK, D])
    NS = cfg.get("NS", 2)
    swa_w_in = din("swa_w_in", [NS, D, SWA_IN])
    C.swa_sinks = din("swa_sinks", [NS, 16])
    swa_w_out = din("swa_w_out", [NS, D, D])
    C.swa_mask_in = din("swa_mask", [128, 3, 4, 384])
    C.rope_in = din("rope", [NTOK, 256])
    C.swin_s = {}
    C.swout_s = {}
    C.t_of = Tok()
    C.gwin_s = {}
    C.gwout_s = {}
    y = nc.dram_tensor("y", [NTOK, D], F32, kind="ExternalOutput").ap()
    xs = [dscr("xs0", [NTOK, D]), dscr("xs1", [NTOK, D])]
    C.t_wscr = Tok()
    C.t_xsrc = Tok()
    C.t_xdst = Tok()
    setup_globals(nc, P, C)
    phases = cfg["phases"]
    w1s = {}
    w2s = {}
    pjobs = []
    wtok = [Tok() for _ in phases]
    for pi, ph in enumerate(phases):
        jobs = []
        if ph[0] == "ffn":
            _, li, k = ph
            w1s[(li, k)] = dscr(f"w1s_{li}_{k}", [NFC, 128, 8, 256], BF16)
            w2s[(li, k)] = dscr(f"w2s_{li}_{k}", [128, NFC, 1024], BF16)
            jobs += w1_jobs(ffn_w1[li, k], w1s[(li, k)])
            v2 = ffn_w2[li, k].rearrange("(fc p) d -> p fc d", p=128)
            for fc in range(0, NFC, 2):
                jobs.append(([((0, 1024), v2[:, fc:fc + 2, :])], w2s[(li, k)][:, fc:fc + 2, :], "gen"))
        if ph[0] == "gla":
            j = ph[1]
            C.gwin_s[j] = dscr(f"gwin_s{j}", [128, 8, GLA_IN], BF16)
            C.gwout_s[j] = dscr(f"gwout_s{j}", [128, 8, D], BF16)
            jobs += mat_jobs(gla_w_in[j], C.gwin_s[j], GLA_IN)
            jobs += mat_jobs(gla_w_out[j], C.gwout_s[j], D)
        if ph[0] == "swa":
            j = ph[1]
            C.swin_s[j] = dscr(f"swin_s{j}", [128, 8, SWA_IN], BF16)
            C.swout_s[j] = dscr(f"swout_s{j}", [128, 8, D], BF16)
            jobs += mat_jobs(swa_w_in[j], C.swin_s[j], SWA_IN)
            jobs += mat_jobs(swa_w_out[j], C.swout_s[j], D)
        pjobs.append([(srcs, dst_, wtok[pi]) for (srcs, dst_, _k) in jobs])
    ffn_idx = [i for i, ph in enumerate(phases) if ph[0] == "ffn"]
    bg_for = {}
    if cfg.get("bg_conv", True) and ffn_idx:
        first = ffn_idx[0]
        upfront = [j for pj in pjobs[:first + 1] for j in pj]
        for a_, b_ in zip(ffn_idx, ffn_idx[1:] + [len(phases) - 1]):
            bg_for[a_] = [j for pj in pjobs[a_ + 1:b_ + 1] for j in pj]
        if ffn_idx[-1] < len(phases) - 1:
            bg_for[ffn_idx[-1]] = [j for pj in pjobs[ffn_idx[-1] + 1:] for j in pj]
    else:
        upfront = [j for pj in pjobs for j in pj]
    convert_weights(nc, P, C, upfront)
    cur = x_in
    for i, ph in enumerate(phases):
        dst = y if i == len(phases) - 1 else xs[i % 2]
        C.t_wscr = wtok[i]
        if ph[0] == "ffn":
            _, li, k = ph
            ffn_phase(nc, P, C, cur, dst, w1s[(li, k)], w2s[(li, k)], norm_g[li, 0 if k == 0 else 4], norm_g[li, 1 if k == 0 else 5], bg_jobs=bg_for.get(i, ()))
        if ph[0] == "gla":
            _, j, li = ph
            gla_layer(nc, P, C, cur, dst, j, norm_g[li, 2], norm_g[li, 3])
        if ph[0] == "swa":
            _, j, li = ph
            swa_layer(nc, P, C, cur, dst, j, norm_g[li, 2], norm_g[li, 3])
        cur = dst
    P.barrier([C.t_xdst])
    print("deadlock check:", simulate(P)[0])
    print("instructions:", P.nins, "sems:", [e.nsem for e in P.engs], len(P.dma_toks))
    return nc


NCORES = 8
SEQ = 8192
DEC_SEQ = 2048
SEG_TILES = DEC_SEQ // 128


def _consts(chain):
    s_ = np.arange(128)[:, None]
    c_ = np.arange(128)[None, :]
    same = (s_ // 64) == (c_ // 64)
    mf = (same & (s_ <= c_)).astype(np.float32)
    mb = (same & (s_ >= c_)).astype(np.float32)
    mp = np.where(c_ >= s_, 0.0, NEG).astype(np.float32)
    mn = np.where(c_ <= s_, 0.0, NEG).astype(np.float32)
    z = np.zeros((128, 128), np.float32)
    full = np.full((128, 128), NEG, np.float32)
    bp = mp if chain else full
    bn = mn if chain else full
    swa_mask = np.stack([np.concatenate([mp, z, mn], 1), np.concatenate([bp, z, mn], 1), np.concatenate([mp, z, bn], 1)], 1)
    swa_mask = np.repeat(swa_mask[:, :, None, :], 4, axis=2)
    tri2 = np.zeros((128, 16), np.float32)
    tri2[0:64, 0] = -1.0 / 16
    tri2[64:128, 1] = -1.0 / 16
    return dict(ident=np.eye(128).astype(ml_dtypes.bfloat16), tri_f=-mf / 16.0, tri_b=-mb / 16.0, tri2=tri2,
                mask_f=np.tile(mf, (1, 4)), mask_b=np.tile(mb, (1, 4)), swa_mask=np.ascontiguousarray(swa_mask),
                cf=np.full((128, 1), 1.0 if chain else 0.0, np.float32))


def _rope_table(pos):
    inv = (500000.0 ** (-(np.arange(8, dtype=np.float32) * 2.0 / 16))).astype(np.float32)
    ang = pos.astype(np.float32)[:, None] * inv[None, :]
    cos = np.cos(ang).astype(np.float32)
    sin = np.sin(ang).astype(np.float32)
    return np.ascontiguousarray(np.concatenate([np.tile(cos, (1, 16)), np.tile(sin, (1, 16))], 1).astype(np.float32))


def full_phases():
    ph = []
    for li in range(4):
        ph.append(("ffn", li, 0))
        ph.append(("gla", li // 2, li) if li % 2 == 0 else ("swa", li // 2, li))
        ph.append(("ffn", li, 1))
    return ph


def kernel(**inputs):
    f32 = lambda a: np.ascontiguousarray(np.asarray(a, dtype=np.float32))
    xp = f32(inputs["x_prompt"])
    xsm = f32(inputs["x_sample"])
    wnames = ["norm_g", "ffn_w1", "ffn_w2", "gla_w_in", "gla_w_gate_f", "gla_b_gate_f", "gla_w_gate_b", "gla_b_gate_b",
              "gla_onorm", "gla_w_out", "swa_w_in", "swa_sinks", "swa_w_out"]
    wts = {k: f32(inputs[k]) for k in wnames}
    nc = build(dict(NTOK=SEQ, NL=4, NG=2, NS=2, seg_tiles=SEG_TILES, phases=full_phases()))
    cP = _consts(True)
    cS = _consts(False)
    ropeP = _rope_table(np.arange(SEQ))
    ropeS = _rope_table(np.tile(np.arange(DEC_SEQ), SEQ // DEC_SEQ))
    in_maps = []
    for c in range(NCORES):
        m = dict(wts)
        if c < 4:
            m["x"] = xp[c]
            m["rope"] = ropeP
            m.update(cP)
        else:
            k = c - 4
            x = np.zeros((SEQ, D), np.float32)
            x[0:DEC_SEQ] = xsm[2 * k]
            x[DEC_SEQ:2 * DEC_SEQ] = xsm[2 * k + 1]
            m["x"] = x
            m["rope"] = ropeS
            m.update(cS)
        in_maps.append(m)
    res = run_bass_kernel_spmd(nc, in_maps, core_ids=list(range(NCORES)))
    yp = np.stack([np.asarray(res.results[c]["y"], dtype=np.float32) for c in range(4)], 0)
    ys = np.zeros_like(xsm)
    for k in range(4):
        y = np.asarray(res.results[4 + k]["y"], dtype=np.float32)
        ys[2 * k] = y[0:DEC_SEQ]
        ys[2 * k + 1] = y[DEC_SEQ:2 * DEC_SEQ]
    return (yp, ys)
```
